# Optimizing a Trainium2 kernel written in Bass

```python
import jax, jax.numpy as jnp
from jax import lax
import numpy as np

D_MODEL = 1024
BATCH = 2
SEQ = 8192
DEPTH = 1

N_HEADS_A = 8
N_KV_A = 2
HEAD_DIM_A = 64
WINDOW = 128
BLOCK = 128
N_HEADS_B = 8
QK_NOPE = 64
QK_ROPE = 32
V_DIM_B = 64
Q_LORA = 256
KV_LORA = 128
ROPE_THETA = 10000.0
D_FF = 4 * D_MODEL
EPS = 1e-6

WIDTH_A = N_HEADS_A * HEAD_DIM_A
WIDTH_B = N_HEADS_B * V_DIM_B
KV_WIDTH_A = N_KV_A * HEAD_DIM_A
Q_HEAD_B = QK_NOPE + QK_ROPE
KV_HEAD_B = QK_NOPE + V_DIM_B
SPLITS = (D_MODEL, D_MODEL, WIDTH_A, KV_WIDTH_A, KV_WIDTH_A, Q_LORA, KV_LORA, QK_ROPE)
D_IN = int(sum(SPLITS))
SPLIT_IDX = tuple(int(i) for i in np.cumsum(SPLITS)[:-1])

kernel_name = "hybrid_swa_sink_alibi_mla_gated_sqrelu"


def rmsnorm(x, g):
    x32 = x.astype(jnp.float32)
    y = x32 * lax.rsqrt(jnp.mean(x32 * x32, axis=-1, keepdims=True) + EPS)
    return y.astype(x.dtype) * g


def alibi_slopes(n):
    return 2.0 ** (-8.0 * jnp.arange(1, n + 1, dtype=jnp.float32) / n)


def rope(x, pos):
    d = x.shape[-1]
    freqs = ROPE_THETA ** (-jnp.arange(0, d, 2, dtype=jnp.float32) / d)
    ang = pos.astype(jnp.float32)[..., None] * freqs
    cos, sin = jnp.cos(ang)[:, :, None, :], jnp.sin(ang)[:, :, None, :]
    x32 = x.astype(jnp.float32)
    x1, x2 = x32[..., : d // 2], x32[..., d // 2:]
    return jnp.concatenate([x1 * cos - x2 * sin, x2 * cos + x1 * sin], axis=-1).astype(x.dtype)


def swa_sink_alibi_attention(q, k, v, pos, sinks):
    B, S = q.shape[0], q.shape[1]
    nb = S // BLOCK
    G = N_HEADS_A // N_KV_A
    qb = q.reshape(B, nb, BLOCK, N_KV_A, G, HEAD_DIM_A)

    def band(t):
        padded = jnp.pad(t, [(0, 0), (BLOCK, 0)] + [(0, 0)] * (t.ndim - 2))
        prev = padded[:, :S].reshape((B, nb, BLOCK) + t.shape[2:])
        cur = t.reshape((B, nb, BLOCK) + t.shape[2:])
        return jnp.concatenate([prev, cur], axis=2)

    kb, vb, pb = band(k), band(v), band(pos)
    qpos = pos.reshape(B, nb, BLOCK)
    scale = HEAD_DIM_A ** -0.5
    s = jnp.einsum('bnqkgd,bnskd->bnkgqs', qb, kb).astype(jnp.float32) * scale
    dist = jnp.abs(qpos[:, :, :, None] - pb[:, :, None, :]).astype(jnp.float32)
    slopes = alibi_slopes(N_HEADS_A).reshape(N_KV_A, G)
    s = s - slopes[None, None, :, :, None, None] * dist[:, :, None, None]
    qi = jnp.arange(BLOCK)[:, None] + BLOCK
    si = jnp.arange(2 * BLOCK)[None, :]
    diff = qi - si
    valid = (diff >= 0) & (diff < WINDOW)
    not_pad = (jnp.arange(nb)[:, None, None] > 0) | (si[None] >= BLOCK)
    mask = valid[None] & not_pad
    s = jnp.where(mask[None, :, None, None], s, -jnp.inf)
    sink = sinks.astype(jnp.float32).reshape(1, 1, N_KV_A, G, 1, 1)
    m = jnp.maximum(jnp.max(s, axis=-1, keepdims=True), sink)
    e = jnp.exp(s - m)
    p = e / (jnp.sum(e, axis=-1, keepdims=True) + jnp.exp(sink - m))
    o = jnp.einsum('bnkgqs,bnskd->bnqkgd', p.astype(v.dtype), vb)
    return o.reshape(B, S, WIDTH_A)


def mla_attention(q_nope, q_rope, k_nope, k_rope, v):
    B, S = q_nope.shape[0], q_nope.shape[1]
    nb = S // BLOCK
    scale = Q_HEAD_B ** -0.5
    qn = q_nope.reshape(B, nb, BLOCK, N_HEADS_B, QK_NOPE).transpose(1, 0, 2, 3, 4)
    qr = q_rope.reshape(B, nb, BLOCK, N_HEADS_B, QK_ROPE).transpose(1, 0, 2, 3, 4)
    kidx = jnp.arange(S)

    def one_block(args):
        qn_b, qr_b, i = args
        s = (jnp.einsum('bqhd,bshd->bhqs', qn_b, k_nope)
             + jnp.einsum('bqhd,bsd->bhqs', qr_b, k_rope)).astype(jnp.float32) * scale
        qidx = i * BLOCK + jnp.arange(BLOCK)
        s = jnp.where(kidx[None, :] <= qidx[:, None], s, -jnp.inf)
        p = jax.nn.softmax(s, axis=-1)
        return jnp.einsum('bhqs,bshd->bqhd', p.astype(v.dtype), v)

    o = lax.map(one_block, (qn, qr, jnp.arange(nb)))
    return o.transpose(1, 0, 2, 3, 4).reshape(B, S, WIDTH_B)


def setup_inputs(seed: int = 0) -> dict:
    key = jax.random.key(seed)
    ks = jax.random.split(key, 20)

    def w(k, shape, fan_in):
        return jax.random.normal(k, shape, jnp.float32) * fan_in ** -0.5

    def gain(k, n):
        return 1.0 + 0.02 * jax.random.normal(k, (DEPTH, n), jnp.float32)

    x = jax.random.normal(ks[0], (BATCH, SEQ, D_MODEL), jnp.float32)
    offset = jax.random.randint(ks[1], (BATCH, 1), 0, 1024, dtype=jnp.int32)
    positions = (offset + jnp.arange(SEQ, dtype=jnp.int32)[None, :]).astype(jnp.int32)
    return {
        "x": x,
        "positions": positions,
        "pre_norm_mix": gain(ks[2], D_MODEL),
        "w_in": w(ks[3], (DEPTH, D_MODEL, D_IN), D_MODEL),
        "q_a_norm": gain(ks[4], Q_LORA),
        "w_q_b": w(ks[5], (DEPTH, Q_LORA, N_HEADS_B * Q_HEAD_B), Q_LORA),
        "kv_a_norm": gain(ks[6], KV_LORA),
        "w_kv_b": w(ks[7], (DEPTH, KV_LORA, N_HEADS_B * KV_HEAD_B), KV_LORA),
        "sinks": jax.random.normal(ks[8], (DEPTH, N_HEADS_A), jnp.float32),
        "w_o_a": w(ks[9], (DEPTH, WIDTH_A, D_MODEL), WIDTH_A),
        "w_o_b": w(ks[10], (DEPTH, WIDTH_B, D_MODEL), WIDTH_B),
        "w_out": w(ks[11], (DEPTH, D_MODEL, D_MODEL), D_MODEL),
        "post_norm_mix": gain(ks[12], D_MODEL),
        "pre_norm_mlp": gain(ks[13], D_MODEL),
        "w_up": w(ks[14], (DEPTH, D_MODEL, D_FF), D_MODEL),
        "w_down": w(ks[15], (DEPTH, D_FF, D_MODEL), D_FF),
        "post_norm_mlp": gain(ks[16], D_MODEL),
    }


def reference(x, positions, pre_norm_mix, w_in, q_a_norm, w_q_b, kv_a_norm, w_kv_b, sinks,
              w_o_a, w_o_b, w_out, post_norm_mix, pre_norm_mlp, w_up, w_down, post_norm_mlp):
    B, S = x.shape[0], x.shape[1]
    for l in range(DEPTH):
        h = rmsnorm(x, pre_norm_mix[l])
        proj = h @ w_in[l]
        g_a, g_b, qa, ka, va, cq, ckv, kr = jnp.split(proj, SPLIT_IDX, axis=-1)
        qa = qa.reshape(B, S, N_HEADS_A, HEAD_DIM_A)
        ka = ka.reshape(B, S, N_KV_A, HEAD_DIM_A)
        va = va.reshape(B, S, N_KV_A, HEAD_DIM_A)
        out_a = swa_sink_alibi_attention(qa, ka, va, positions, sinks[l])
        qb = (rmsnorm(cq, q_a_norm[l]) @ w_q_b[l]).reshape(B, S, N_HEADS_B, Q_HEAD_B)
        kvb = (rmsnorm(ckv, kv_a_norm[l]) @ w_kv_b[l]).reshape(B, S, N_HEADS_B, KV_HEAD_B)
        q_nope, q_rope = qb[..., :QK_NOPE], rope(qb[..., QK_NOPE:], positions)
        k_nope, v_b = kvb[..., :QK_NOPE], kvb[..., QK_NOPE:]
        k_rope = rope(kr[:, :, None, :], positions)[:, :, 0, :]
        out_b = mla_attention(q_nope, q_rope, k_nope, k_rope, v_b)
        merged = jax.nn.sigmoid(g_a) * (out_a @ w_o_a[l]) + jax.nn.sigmoid(g_b) * (out_b @ w_o_b[l])
        x = x + rmsnorm(merged @ w_out[l], post_norm_mix[l])
        h2 = rmsnorm(x, pre_norm_mlp[l])
        y = jnp.square(jax.nn.relu(h2 @ w_up[l])) @ w_down[l]
        x = x + rmsnorm(y, post_norm_mlp[l])
    return x
```

```python
import contextlib
import math
import numpy as np
import concourse.bass as bass
import concourse.mybir as mybir
from concourse.bass_utils import run_bass_kernel_spmd

F32 = mybir.dt.float32
BF16 = mybir.dt.bfloat16
I32 = mybir.dt.int32
AF = mybir.ActivationFunctionType
ALU = mybir.AluOpType

D = 1024
DFF = 4096
EPS = 1e-6
NEG = -30000.0
BIG = 1.0e5
SCALE_A = 64 ** -0.5
SCALE_B = 96 ** -0.5
PI = math.pi
DSZ = {F32: 4, BF16: 2, I32: 4}
ARENA_BYTES = 195 * 1024
IARENA_ELEMS = 2048 + 1024 + 64 + 64
EPOCH = 30000


class Buf:
    __slots__ = ("name", "w", "r", "dsem", "dcnt")

    def __init__(self, name):
        self.name = name
        self.w = None
        self.r = {}
        self.dsem = None
        self.dcnt = 0


class T:
    def __init__(self, ap, name):
        self.ap = ap
        self.b = Buf(name)


class Eng:
    def __init__(self, kern, name, h, is_pe=False):
        self.kern = kern
        self.name = name
        self.h = h
        self.is_pe = is_pe
        self.sem = kern.new_sem(name)
        self.cnt = 0
        self.seen = {}
        self.pending = False

    def wait(self, tok):
        sem, val, _ = tok
        if self.seen.get(sem, 0) >= val:
            return
        self.h.wait_ge(sem, val)
        self.seen[sem] = val


class Kern:
    def __init__(self):
        self.nc = bass.Bass("TRN2", target_bir_lowering=False)
        self.es = contextlib.ExitStack()
        nc = self.nc
        self.nsem = 0
        self.arena = self.es.enter_context(nc.sbuf_tensor("arena", [128, ARENA_BYTES // 4], F32))
        self.top = 0
        self.iarena = self.es.enter_context(nc.sbuf_tensor("iarena", [128, IARENA_ELEMS], I32))
        self.banks = []
        for i in range(8):
            t = self.es.enter_context(nc.psum_tensor("bank%d" % i, [128, 512], F32))
            self.banks.append(T(t[:], "bank%d" % i))
        self.E = {
            "pe": Eng(self, "pe", nc.tensor, True),
            "act": Eng(self, "act", nc.scalar),
            "dve": Eng(self, "dve", nc.vector),
            "pool": Eng(self, "pool", nc.gpsimd),
            "sp": Eng(self, "sp", nc.sync),
        }
        self.dma_sems = {}
        self.sem_cnt = {}
        self.free_dsems = []
        self.dma_owners = []
        self.uid = 0

    def new_sem(self, name):
        self.nsem += 1
        return self.es.enter_context(self.nc.semaphore("s%d_%s" % (self.nsem, name)))

    def mark(self):
        return self.top

    def release(self, mark):
        self.top = mark

    def ialloc(self, shape, off, name):
        n = int(np.prod(shape[1:]))
        assert off + n <= IARENA_ELEMS
        ap = self.iarena[:, off:off + n]
        if len(shape) == 3:
            ap = ap.rearrange("p (a b) -> p a b", a=shape[1])
        self.uid += 1
        return T(ap, "%s_%d" % (name, self.uid))

    def alloc(self, shape, dt, name):
        assert dt != I32
        n = int(np.prod(shape[1:]))
        nbytes = (n * DSZ[dt] + 63) // 64 * 64
        off = self.top
        self.top += nbytes
        assert self.top <= ARENA_BYTES, "SBUF arena overflow at %s: %d" % (name, self.top)
        ap = self.arena[:, off // 4:(off + nbytes) // 4]
        if dt != F32:
            ap = ap.bitcast(dt)
        ap = ap[:, 0:n]
        if len(shape) == 3:
            ap = ap.rearrange("p (a b) -> p a b", a=shape[1])
        elif len(shape) == 4:
            ap = ap.rearrange("p (a b c) -> p a b c", a=shape[1], b=shape[2])
        elif len(shape) == 5:
            ap = ap.rearrange("p (a b c d) -> p a b c d", a=shape[1], b=shape[2], c=shape[3])
        self.uid += 1
        return T(ap, "%s_%d" % (name, self.uid))

    def _bufs(self, lst):
        return [x.b if isinstance(x, T) else x for x in lst]

    def op(self, en, fn, r=(), w=(), inc=True):
        eng = self.E[en]
        r = self._bufs(r)
        w = self._bufs(w)
        toks = []
        for b in r:
            if b.w is not None:
                toks.append(b.w)
        for b in w:
            if b.w is not None:
                toks.append(b.w)
            toks.extend(b.r.values())
        for tok in toks:
            if eng.is_pe and tok[2] is eng:
                continue
            eng.wait(tok)
        ins = fn(eng.h)
        if inc:
            if eng.cnt >= EPOCH:
                assert not eng.pending
                eng.sem = self.new_sem(eng.name)
                eng.cnt = 0
            eng.cnt += 1
            ins.then_inc(eng.sem, 1)
            eng.pending = False
            tok = (eng.sem, eng.cnt, eng)
        else:
            assert eng.cnt + 1 <= EPOCH
            eng.pending = True
            tok = (eng.sem, eng.cnt + 1, eng)
        for b in r:
            b.r[tok[0]] = tok
        for b in w:
            b.w = tok
            b.r = {}
        return ins

    def dma(self, q, out, in_, r=(), w=(), owner=None, **kw):
        eng = self.E[q]
        r = self._bufs(r)
        w = self._bufs(w)
        toks = []
        for b in r:
            if b.w is not None:
                toks.append(b.w)
        for b in w:
            if b.w is not None:
                toks.append(b.w)
            toks.extend(b.r.values())
        for tok in toks:
            eng.wait(tok)
        if owner is None:
            owner = w[0] if w else r[0]
        elif isinstance(owner, T):
            owner = owner.b
        if owner.dsem is None:
            if self.free_dsems:
                owner.dsem = self.free_dsems.pop()
            else:
                owner.dsem = self.new_sem("dma")
                self.sem_cnt[owner.dsem] = 0
            self.dma_owners.append(owner)
        self.sem_cnt[owner.dsem] += 16
        ins = eng.h.dma_start(out=out, in_=in_, **kw)
        ins.then_inc(owner.dsem, 16)
        tok = (owner.dsem, self.sem_cnt[owner.dsem], None)
        self.dma_sems[owner.dsem] = tok
        for b in r:
            b.r[tok[0]] = tok
        for b in w:
            b.w = tok
            b.r = {}
        return ins

    def barrier(self):
        sp = self.E["sp"]
        for e in self.E.values():
            assert not e.pending
            if e is not sp and e.cnt > 0:
                sp.wait((e.sem, e.cnt, e))
        for tok in self.dma_sems.values():
            sp.wait(tok)
        if sp.cnt >= EPOCH:
            sp.sem = self.new_sem("sp")
            sp.cnt = 0
        sp.cnt += 1
        sp.h.nop().then_inc(sp.sem, 1)
        btok = (sp.sem, sp.cnt, sp)
        allseen = dict(sp.seen)
        allseen[sp.sem] = sp.cnt
        for e in self.E.values():
            if e is not sp:
                e.wait(btok)
        for o in self.dma_owners:
            self.free_dsems.append(o.dsem)
            o.dsem = None
        self.dma_owners = []
        for e in self.E.values():
            for s, v in allseen.items():
                if e.seen.get(s, 0) < v:
                    e.seen[s] = v
            for e2 in self.E.values():
                if e.seen.get(e2.sem, 0) < e2.cnt:
                    e.seen[e2.sem] = e2.cnt

    def finish(self):
        self.barrier()
        self.es.close()


def build(S):
    NB = S // 128
    NQ = NB // 4
    NT = NQ * 128
    GT = min(4, NQ)
    NG = NQ // GT
    GN = GT * 128
    NLG = S // 512

    K = Kern()
    nc = K.nc
    op = K.op
    B = K.banks

    def din(name, shape, dt=F32):
        return nc.dram_tensor(name, list(shape), dt, kind="ExternalInput").ap()

    xb = din("xb", [S, D])
    xo = din("xo", [NT, D])
    xp = din("xp", [NT, D])
    posbc = din("posbc", [128, NB], I32)
    posoc = din("posoc", [128, NQ], I32)
    poso = din("poso", [NT], I32)
    posk = din("posk", [128, 2 * NQ], I32)
    mlamask = din("mlamask", [128, 4, 128])
    swamask = din("swamask", [128, 3, 128])
    freqrow = din("freqrow", [128, 16])
    nslope = din("nslope", [128, 8])
    w_in = din("w_in", [D, 3232])
    w_q_b = din("w_q_b", [256, 768])
    w_kv_b = din("w_kv_b", [128, 1024])
    w_o_a = din("w_o_a", [512, D])
    w_o_b = din("w_o_b", [512, D])
    w_out = din("w_out", [D, D])
    w_up = din("w_up", [D, DFF])
    w_down = din("w_down", [DFF, D])
    g_pre_mix = din("pre_norm_mix", [D])
    g_qa = din("q_a_norm", [256])
    g_kva = din("kv_a_norm", [128])
    sinks = din("sinks", [8])
    g_post_mix = din("post_norm_mix", [D])
    g_pre_mlp = din("pre_norm_mlp", [D])
    g_post_mlp = din("post_norm_mlp", [D])
    out = nc.dram_tensor("out", [NT, D], F32, kind="ExternalOutput").ap()

    ident_f = K.alloc([128, 128], F32, "ident_f")
    ident_b = K.alloc([128, 128], BF16, "ident_b")
    ones_f = K.alloc([128, 128], F32, "ones_f")
    g1c = K.alloc([128, 8], F32, "g1c")
    g3c = K.alloc([128, 8], F32, "g3c")
    gqc = K.alloc([128, 2], F32, "gqc")
    gkc = K.alloc([128, 1], F32, "gkc")
    g2b = K.alloc([128, D], F32, "g2b")
    g4b = K.alloc([128, D], F32, "g4b")
    F_BASE = K.mark()
    nsl = K.alloc([128, 8], F32, "nsl")
    frq = K.alloc([128, 16], F32, "frq")
    sinkrow = K.alloc([128, 8], F32, "sinkrow")
    sinkexp = K.alloc([128, 8, 128], F32, "sinkexp")
    mmask_f = K.alloc([128, 4, 128], F32, "mmask_f")
    mmask = K.alloc([128, 4, 128], BF16, "mmask")
    smask = K.alloc([128, 3, 128], F32, "smask")
    posk_i = K.ialloc([128, 2 * NQ], 2048 + 1024 + 64, "posk_i")
    nposk = K.alloc([128, 2 * NQ], F32, "nposk")
    CONST_MARK = K.mark()

    cl = [
        (g1c, g_pre_mix.rearrange("(k p) -> p k", p=128), True),
        (g3c, g_pre_mlp.rearrange("(k p) -> p k", p=128), True),
        (gqc, g_qa.rearrange("(k p) -> p k", p=128), True),
        (gkc, g_kva.rearrange("(k p) -> p k", p=128), True),
        (g2b, g_post_mix.partition_broadcast(128), False),
        (g4b, g_post_mlp.partition_broadcast(128), False),
        (nsl, nslope, False),
        (frq, freqrow, False),
        (sinkrow, sinks.partition_broadcast(128), False),
        (mmask_f, mlamask, False),
        (smask, swamask, False),
        (posk_i, posk, False),
    ]
    for t, src, slow in cl:
        if slow:
            K.dma("sp", t.ap, src, w=[t], allow_slow_non_contiguous=True)
        else:
            K.dma("sp", t.ap, src, w=[t])
    op("pool", lambda e: e.memset(ones_f.ap, 1.0), w=[ones_f])
    op("pool", lambda e: e.memset(ident_f.ap, 1.0), w=[ident_f])
    op("pool", lambda e: e.affine_select(out=ident_f.ap, in_=ident_f.ap, pattern=[[-1, 128]],
                                         compare_op=ALU.is_equal, fill=0.0, base=0, channel_multiplier=1),
       r=[ident_f], w=[ident_f])
    op("dve", lambda e: e.tensor_copy(out=ident_b.ap, in_=ident_f.ap), r=[ident_f], w=[ident_b])
    op("dve", lambda e: e.tensor_copy(out=mmask.ap, in_=mmask_f.ap), r=[mmask_f], w=[mmask])
    op("act", lambda e: e.activation(out=sinkrow.ap, in_=sinkrow.ap, func=AF.Exp), r=[sinkrow], w=[sinkrow])
    op("dve", lambda e: e.tensor_copy(out=sinkexp.ap, in_=sinkrow.ap[:, :, None].to_broadcast([128, 8, 128])),
       r=[sinkrow], w=[sinkexp])
    op("dve", lambda e: e.tensor_copy(out=nposk.ap, in_=posk_i.ap), r=[posk_i], w=[nposk])
    op("dve", lambda e: e.tensor_scalar(out=nposk.ap, in0=nposk.ap, scalar1=-1.0, scalar2=None, op0=ALU.mult),
       r=[nposk], w=[nposk])
    K.barrier()

    def load_w(dst, dram, r0, KT, c0, C, gain=None, chunk=256, stage=None, extra_scale=None, engs=("dve",),
               defer=False, ctr=None):
        if defer:
            out_ = []
            for cc0 in range(0, C, chunk):
                cc = min(chunk, C - cc0)

                def f(cc0=cc0, cc=cc):
                    st = stage[ctr[0] % len(stage)]
                    ctr[0] += 1
                    sub = T(dst.ap[:, :, cc0:cc0 + cc], "sub")
                    sub.b = dst.b
                    load_w(sub, dram, r0, KT, c0 + cc0, cc, gain=gain, chunk=cc, stage=[st], extra_scale=extra_scale)
                out_.append(f)
            return out_
        i = 0
        for cc0 in range(0, C, chunk):
            cc = min(chunk, C - cc0)
            st = stage[i % len(stage)]
            src = dram[r0:r0 + KT * 128, c0 + cc0:c0 + cc0 + cc].rearrange("(k p) c -> p k c", p=128)
            K.dma("sp", st.ap[:, 0:KT, 0:cc], src, w=[st])
            en = engs[i % len(engs)]
            o = dst.ap[:, 0:KT, cc0:cc0 + cc]
            s_ap = st.ap[:, 0:KT, 0:cc]
            if gain is None:
                if extra_scale is None:
                    op(en, lambda e, o=o, s_ap=s_ap: e.tensor_copy(out=o, in_=s_ap), r=[st], w=[dst])
                else:
                    op("dve", lambda e, o=o, s_ap=s_ap: e.tensor_scalar(out=o, in0=s_ap, scalar1=extra_scale, scalar2=None,
                                                                  op0=ALU.mult), r=[st], w=[dst])
            else:
                gb = gain.ap[:, 0:KT, None].to_broadcast([128, KT, cc])
                if extra_scale is None:
                    op(en, lambda e, o=o, s_ap=s_ap, gb=gb: e.tensor_tensor(out=o, in0=s_ap, in1=gb, op=ALU.mult),
                       r=[st, gain], w=[dst])
                else:
                    op("dve", lambda e, o=o, s_ap=s_ap, gb=gb: e.scalar_tensor_tensor(out=o, in0=s_ap, scalar=extra_scale,
                                                                                in1=gb, op0=ALU.mult, op1=ALU.mult),
                       r=[st, gain], w=[dst])
            i += 1

    def nt_front(srcs, xts, xss, stat, pieces=None, defer_dma=False):
        n = len(srcs)
        assert len(xss) >= n and (xts is None or len(xts) >= n)
        def emit(f):
            if pieces is None:
                f()
            else:
                pieces.append(f)

        xin = []
        for i, s in enumerate(srcs):
            if s[0] == "dram":
                xt = xts[i % len(xts)]
                if defer_dma and pieces is not None:
                    emit(lambda xt=xt, src=s[1]: K.dma("sp", xt.ap, src, w=[xt]))
                else:
                    K.dma("sp", xt.ap, s[1], w=[xt])
                xin.append((xt.ap, xt))
            else:
                xin.append((s[1], s[2]))

        for i in range(n):
            xs = xss[i % len(xss)]
            a, t = xin[i]
            emit(lambda a=a, xs=xs, i=i, t=t: op("act", lambda e: e.activation(
                out=xs.ap, in_=a, func=AF.Square, accum_out=stat.ap[:, i:i + 1]), r=[t], w=[xs, stat]))

        def rstd():
            op("act", lambda e: e.activation(out=stat.ap[:, n:2 * n], in_=stat.ap[:, 0:n], func=AF.Ln,
                                             bias=EPS, scale=1.0 / D), r=[stat], w=[stat])
            op("act", lambda e: e.activation(out=stat.ap[:, n:2 * n], in_=stat.ap[:, n:2 * n], func=AF.Exp, scale=-0.5),
               r=[stat], w=[stat])
        emit(rstd)
        for i in range(n):
            xs = xss[i % len(xss)]
            a, t = xin[i]
            emit(lambda a=a, xs=xs, i=i, t=t: op("dve", lambda e: e.tensor_scalar(
                out=xs.ap, in0=a, scalar1=stat.ap[:, n + i:n + i + 1], scalar2=None, op0=ALU.mult), r=[t, stat], w=[xs]))
        return n

    def nt_back(n, hT, xss, pbanks):
        for i in range(n):
            xs = xss[i % len(xss)]
            pb = pbanks[i % len(pbanks)]
            pbv = pb.ap.bitcast(BF16)
            for k in range(8):
                op("pe", lambda e, k=k, xs=xs, pbv=pbv: e.transpose(out=pbv[:, k * 128:(k + 1) * 128],
                                                                   in_=xs.ap[:, k * 128:(k + 1) * 128],
                                                                   identity=ident_b.ap),
                   r=[xs, ident_b], w=[pb], inc=(k == 7))
            o_ap = hT.ap[:, :, i * 128:(i + 1) * 128]
            i_ap = pbv.rearrange("p (k n) -> p k n", k=8)
            if i % 2 == 0:
                op("act", lambda e, o_ap=o_ap, i_ap=i_ap: e.activation(out=o_ap, in_=i_ap, func=AF.Copy), r=[pb], w=[hT])
            else:
                op("dve", lambda e, o_ap=o_ap, i_ap=i_ap: e.tensor_copy(out=o_ap, in_=i_ap), r=[pb], w=[hT])

    def norm_transpose(srcs, hT, xts, xss, pbanks, stat):
        n = nt_front(srcs, xts, xss, stat)
        nt_back(n, hT, xss, pbanks)

    def rope_tables_tokmajor(pos_cols_dram, ntile, ct, st_):
        m0 = K.mark()
        pi_ = K.ialloc([128, ntile], 2048 + 1024, "pos_i")
        pf = K.alloc([128, ntile], F32, "pos_f")
        ang = K.alloc([128, ntile, 16], F32, "ang")
        a2 = K.alloc([128, ntile, 16], F32, "a2")
        u = K.alloc([128, ntile, 16], F32, "u")
        ki = K.ialloc([128, ntile, 16], 2048, "ki")
        kf = K.alloc([128, ntile, 16], F32, "kf")
        K.dma("sp", pi_.ap, pos_cols_dram, w=[pi_])
        op("dve", lambda e: e.tensor_copy(out=pf.ap, in_=pi_.ap), r=[pi_], w=[pf])
        op("dve", lambda e: e.tensor_tensor(out=ang.ap, in0=pf.ap[:, :, None].to_broadcast([128, ntile, 16]),
                                            in1=frq.ap[:, None, :].to_broadcast([128, ntile, 16]), op=ALU.mult),
           r=[pf, frq], w=[ang])
        for dst, shift in ((st_, 0.0), (ct, PI / 2)):
            op("dve", lambda e: e.tensor_scalar(out=a2.ap, in0=ang.ap, scalar1=shift, scalar2=None, op0=ALU.add),
               r=[ang], w=[a2])
            op("dve", lambda e: e.tensor_scalar(out=u.ap, in0=a2.ap, scalar1=1.0 / (2 * PI), scalar2=None, op0=ALU.mult),
               r=[a2], w=[u])
            op("dve", lambda e: e.tensor_copy(out=ki.ap, in_=u.ap), r=[u], w=[ki])
            op("dve", lambda e: e.tensor_copy(out=kf.ap, in_=ki.ap), r=[ki], w=[kf])
            op("dve", lambda e: e.scalar_tensor_tensor(out=a2.ap, in0=kf.ap, scalar=-2 * PI, in1=a2.ap, op0=ALU.mult,
                                                       op1=ALU.add), r=[kf, a2], w=[a2])
            op("dve", lambda e: e.tensor_scalar(out=u.ap, in0=a2.ap, scalar1=PI, scalar2=-2 * PI, op0=ALU.is_gt,
                                                op1=ALU.mult), r=[a2], w=[u])
            op("dve", lambda e: e.tensor_tensor(out=a2.ap, in0=a2.ap, in1=u.ap, op=ALU.add), r=[a2, u], w=[a2])
            op("dve", lambda e: e.tensor_scalar(out=u.ap, in0=a2.ap, scalar1=-PI, scalar2=2 * PI, op0=ALU.is_lt,
                                                op1=ALU.mult), r=[a2], w=[u])
            op("dve", lambda e: e.tensor_tensor(out=a2.ap, in0=a2.ap, in1=u.ap, op=ALU.add), r=[a2, u], w=[a2])
            op("dve", lambda e: e.tensor_scalar(out=a2.ap, in0=a2.ap, scalar1=PI, scalar2=-PI, op0=ALU.min, op1=ALU.max),
               r=[a2], w=[a2])
            op("act", lambda e, dst=dst: e.activation(out=dst.ap[:, :, 0:16], in_=a2.ap, func=AF.Sin), r=[a2], w=[dst])
            op("dve", lambda e, dst=dst: e.tensor_copy(out=dst.ap[:, :, 16:32], in_=dst.ap[:, :, 0:16]), r=[dst], w=[dst])
        K.barrier()
        K.release(m0)

    def table_T(ct, st_, t0, nt, cosd, sind, pb_c, pb_s):
        for (src, dst, pb) in ((ct, cosd, pb_c), (st_, sind, pb_s)):
            for i in range(nt):
                op("pe", lambda e, src=src, pb=pb, i=i: e.matmul(pb.ap[0:32, i * 128:(i + 1) * 128],
                                                                lhsT=src.ap[:, t0 + i, :], rhs=ident_f.ap,
                                                                start=True, stop=True),
                   r=[src, ident_f], w=[pb], inc=(i == nt - 1))
            op("act", lambda e, dst=dst, pb=pb: e.activation(out=dst.ap[0:32, 0:nt * 128], in_=pb.ap[0:32, 0:nt * 128],
                                                            func=AF.Copy), r=[pb], w=[dst])

    def rms_feature_major(ps_list, nfeat, N, dst_list, sq, c32, ssb, rr, pb_ss):
        n = len(ps_list)
        for i, ps in enumerate(ps_list):
            op("act", lambda e, ps=ps, i=i: e.activation(out=sq.ap[:, i, 0:N], in_=ps.ap[:, 0:N], func=AF.Square),
               r=[ps], w=[sq])
        for i in range(n):
            op("pe", lambda e, i=i: e.matmul(pb_ss.ap[:, 0:N], lhsT=ones_f.ap, rhs=sq.ap[:, i, 0:N], start=(i == 0),
                                             stop=(i == n - 1)), r=[ones_f, sq], w=[pb_ss], inc=(i == n - 1))
        op("act", lambda e: e.activation(out=ssb.ap[:, 0:N], in_=pb_ss.ap[:, 0:N], func=AF.Ln, bias=EPS,
                                         scale=1.0 / nfeat), r=[pb_ss], w=[ssb])
        op("act", lambda e: e.activation(out=rr.ap[:, 0:N], in_=ssb.ap[:, 0:N], func=AF.Exp, scale=-0.5),
           r=[ssb], w=[rr])
        for i in range(n):
            d_ap, d_t = dst_list[i]
            ps = ps_list[i]
            op("dve", lambda e, i=i, d_ap=d_ap, ps=ps: e.tensor_tensor(out=d_ap, in0=ps.ap[:, 0:N], in1=rr.ap[:, 0:N],
                                                                      op=ALU.mult), r=[ps, rr], w=[d_t])

    outA = K.alloc([128, 4, NT], BF16, "outA")
    outB = K.alloc([128, 4, NT], BF16, "outB")
    cqn = K.alloc([128, 2, NT], BF16, "cqn")
    ctO = K.alloc([128, NQ, 32], F32, "ctO")
    stO = K.alloc([128, NQ, 32], F32, "stO")
    MIX_MARK = K.mark()

    rope_tables_tokmajor(posoc, NQ, ctO, stO)

    m0 = K.mark()
    wproj = K.alloc([128, 8, 1024], BF16, "wproj")
    posq = K.alloc([128, NT], F32, "posq")
    mk_ = K.mark()
    stage = [K.alloc([128, 8, 256], F32, "stage") for _ in range(2)]
    posq_i = K.ialloc([128, NT], 0, "posq_i")
    load_w(wproj, w_in, 0, 8, 2048, 1024, gain=g1c, chunk=256, stage=stage)
    K.dma("sp", posq_i.ap, poso.partition_broadcast(128), w=[posq_i])
    op("dve", lambda e: e.tensor_copy(out=posq.ap, in_=posq_i.ap), r=[posq_i], w=[posq])
    K.barrier()
    K.release(mk_)
    QA0, KA0, VA0, CQ0 = 0, 512, 640, 768
    xts = [K.alloc([128, D], F32, "xt") for _ in range(4)]
    xss = [K.alloc([128, D], BF16, "xs") for _ in range(8)]
    statO = K.alloc([128, 2 * GT], F32, "statO")
    statP = K.alloc([128, 2 * GT], F32, "statP")
    hTo = K.alloc([128, 8, GN], BF16, "hTo")
    hTp = K.alloc([128, 8, GN], BF16, "hTp")
    sq = K.alloc([128, 2, GN], F32, "sq")
    c32 = None
    ssb = K.alloc([128, GN], F32, "ssb")
    rr = K.alloc([128, GN], F32, "rr")
    QaT = K.alloc([128, 8, GN], BF16, "QaT")
    kaT = K.alloc([128, 2, 2, GN], BF16, "kaT")
    vaug = K.alloc([128, 2 * GT * 2, 128], BF16, "vaug")
    dist2 = [K.alloc([128, 2, 128], F32, "dist") for _ in range(2)]
    bias2 = [K.alloc([128, 2, 8, 128], F32, "biasall") for _ in range(2)]
    tS = [K.alloc([128, 512], F32, "tS") for _ in range(4)]
    PTa = [K.alloc([128, 512], BF16, "PTa") for _ in range(4)]
    rrA = [K.alloc([128, 512], F32, "rrA") for _ in range(2)]
    op("pool", lambda e: e.memset(vaug.ap[:, :, 64:128], 1.0), w=[vaug])
    SBA = [B[0], B[1], B[2], B[3]]
    OBA = [B[4], B[5]]

    xtsP = xts
    bd_pieces = []

    def bd_fronts(G, pieces=None):
        t0_ = G * GT
        nO = nt_front([("dram", xo[(t0_ + a) * 128:(t0_ + a + 1) * 128, :]) for a in range(GT)], xts, xss[0:4], statO,
                      pieces)
        nP = nt_front([("dram", xp[(t0_ + a) * 128:(t0_ + a + 1) * 128, :]) for a in range(GT)], xtsP, xss[4:8], statP,
                      pieces, defer_dma=True)
        return nO, nP

    def bd_piece(k=1):
        for _ in range(k):
            if bd_pieces:
                bd_pieces.pop(0)()

    def bd_backs(nO, nP):
        nt_back(nO, hTo, xss[0:4], [B[6], B[7]])
        nt_back(nP, hTp, xss[4:8], [B[6], B[7]])

    bd_backs(*bd_fronts(0))
    for G in range(NG):
        t0 = G * GT
        fr_next = bd_fronts(G + 1, bd_pieces) if G + 1 < NG else None
        for mt in range(2):
            pb = B[2 + mt]
            for k in range(8):
                op("pe", lambda e, k=k, mt=mt, pb=pb: e.matmul(pb.ap[:, 0:GN],
                                                              lhsT=wproj.ap[:, k, CQ0 + mt * 128:CQ0 + (mt + 1) * 128],
                                                              rhs=hTo.ap[:, k, :], start=(k == 0), stop=(k == 7)),
                   r=[wproj, hTo], w=[pb], inc=(k == 7))
        rms_feature_major([B[2], B[3]], 256, GN, [(cqn.ap[:, mt, G * GN:(G + 1) * GN], cqn) for mt in range(2)],
                          sq, c32, ssb, rr, B[4])
        bd_piece(2)
        for h in range(8):
            pb = B[h % 4]
            for k in range(8):
                op("pe", lambda e, k=k, h=h, pb=pb: e.matmul(pb.ap[0:64, 0:GN],
                                                            lhsT=wproj.ap[:, k, QA0 + h * 64:QA0 + (h + 1) * 64],
                                                            rhs=hTo.ap[:, k, :], start=(k == 0), stop=(k == 7)),
                   r=[wproj, hTo], w=[pb], inc=(k == 7))
            if h % 2 == 0:
                op("act", lambda e, h=h, pb=pb: e.activation(out=QaT.ap[0:64, h, :], in_=pb.ap[0:64, 0:GN], func=AF.Copy,
                                                            scale=SCALE_A), r=[pb], w=[QaT])
            else:
                op("dve", lambda e, h=h, pb=pb: e.tensor_scalar(out=QaT.ap[0:64, h, :], in0=pb.ap[0:64, 0:GN],
                                                               scalar1=SCALE_A, scalar2=None, op0=ALU.mult),
                   r=[pb], w=[QaT])
            bd_piece(1)
        i = 0
        for kg in range(2):
            for wch, hT in ((0, hTp), (1, hTo)):
                pb = B[i % 4]
                for k in range(8):
                    op("pe", lambda e, k=k, kg=kg, hT=hT, pb=pb: e.matmul(
                        pb.ap[0:64, 0:GN], lhsT=wproj.ap[:, k, KA0 + kg * 64:KA0 + (kg + 1) * 64], rhs=hT.ap[:, k, :],
                        start=(k == 0), stop=(k == 7)), r=[wproj, hT], w=[pb], inc=(k == 7))
                if i % 2 == 0:
                    op("act", lambda e, kg=kg, wch=wch, pb=pb: e.activation(out=kaT.ap[0:64, kg, wch, :],
                                                                          in_=pb.ap[0:64, 0:GN], func=AF.Copy),
                       r=[pb], w=[kaT])
                else:
                    op("dve", lambda e, kg=kg, wch=wch, pb=pb: e.tensor_copy(out=kaT.ap[0:64, kg, wch, :],
                                                                           in_=pb.ap[0:64, 0:GN]), r=[pb], w=[kaT])
                i += 1
                bd_piece(2)
        for wch, hT in ((0, hTp), (1, hTo)):
            pb = B[4 + wch]
            for a in range(GT):
                for k in range(8):
                    op("pe", lambda e, k=k, a=a, hT=hT, pb=pb: e.matmul(
                        pb.ap[:, a * 128:(a + 1) * 128], lhsT=hT.ap[:, k, a * 128:(a + 1) * 128],
                        rhs=wproj.ap[:, k, VA0:VA0 + 128], start=(k == 0), stop=(k == 7)), r=[wproj, hT], w=[pb],
                       inc=(k == 7 and a == GT - 1))
            o_ap = vaug.ap[:, wch * GT * 2:(wch + 1) * GT * 2, 0:64]
            i_ap = pb.ap[:, 0:GT * 128].rearrange("p (g d) -> p g d", d=64)
            if wch == 0:
                op("act", lambda e, o_ap=o_ap, i_ap=i_ap: e.activation(out=o_ap, in_=i_ap, func=AF.Copy), r=[pb], w=[vaug])
            else:
                op("dve", lambda e, o_ap=o_ap, i_ap=i_ap: e.tensor_copy(out=o_ap, in_=i_ap), r=[pb], w=[vaug])

        its = [(a, kg) for a in range(GT) for kg in range(2)]

        def swa_bias(a):
            m = t0 + a
            ba = bias2[a % 2]
            di = dist2[a % 2]
            for wch in range(2):
                op("act", lambda e: e.activation(out=di.ap[:, wch, :], in_=posq.ap[:, m * 128:(m + 1) * 128],
                                                 func=AF.Abs, bias=nposk.ap[:, 2 * m + wch:2 * m + wch + 1]),
                   r=[posq, nposk], w=[di])
                mi = (0 if m == 0 else 1) if wch == 0 else 2
                op("dve", lambda e: e.tensor_tensor(out=di.ap[:, wch, :], in0=di.ap[:, wch, :], in1=smask.ap[:, mi, :],
                                                    op=ALU.add), r=[di, smask], w=[di])
                op("dve", lambda e: e.tensor_tensor(
                    out=ba.ap[:, wch, :, :], in0=di.ap[:, wch, :][:, None, :].to_broadcast([128, 8, 128]),
                    in1=nsl.ap[:, :, None].to_broadcast([128, 8, 128]), op=ALU.mult), r=[di, nsl], w=[ba])

        def swa_front(idx):
            a, kg = its[idx]
            ba = bias2[a % 2]
            u = idx % 2
            pbO = OBA[u]
            for wch in range(2):
                pbS = SBA[2 * u + wch]
                ts_ = tS[2 * u + wch]
                pt = PTa[2 * u + wch]
                for hh in range(4):
                    op("pe", lambda e, hh=hh: e.matmul(
                        pbS.ap[:, hh * 128:(hh + 1) * 128], lhsT=kaT.ap[0:64, kg, wch, a * 128:(a + 1) * 128],
                        rhs=QaT.ap[0:64, 4 * kg + hh, a * 128:(a + 1) * 128], start=True, stop=True),
                       r=[kaT, QaT], w=[pbS], inc=(hh == 3))
                op("dve", lambda e: e.tensor_tensor(
                    out=ts_.ap, in0=pbS.ap, in1=ba.ap[:, wch, 4 * kg:4 * kg + 4, :].rearrange("p h q -> p (h q)"),
                    op=ALU.add), r=[pbS, ba], w=[ts_])
                op("act", lambda e: e.activation(out=pt.ap, in_=ts_.ap, func=AF.Exp), r=[ts_], w=[pt])

        def swa_PV(idx):
            a, kg = its[idx]
            u = idx % 2
            pbO = OBA[u]
            for wch in range(2):
                pt = PTa[2 * u + wch]
                op("pe", lambda e: e.matmul(pbO.ap, lhsT=vaug.ap[:, (wch * GT + a) * 2 + kg, :], rhs=pt.ap,
                                            start=(wch == 0), stop=(wch == 1)), r=[vaug, pt], w=[pbO], inc=(wch == 1))

        def swa_back(idx):
            a, kg = its[idx]
            u = idx % 2
            pbO = OBA[u]
            r_ = rrA[u]
            tok0 = (t0 + a) * 128
            for hh in range(4):
                op("act", lambda e, hh=hh: e.activation(
                    out=r_.ap[64:128, hh * 128:(hh + 1) * 128], in_=pbO.ap[64:128, hh * 128:(hh + 1) * 128], func=AF.Ln,
                    bias=sinkexp.ap[64:128, 4 * kg + hh, 0:1]), r=[pbO, sinkexp], w=[r_])
            op("act", lambda e: e.activation(out=r_.ap[64:128, :], in_=r_.ap[64:128, :], func=AF.Exp, scale=-1.0),
               r=[r_], w=[r_])
            pO = pbO.ap[0:64, :].rearrange("p (h q) -> p h q", h=4)
            rC = r_.ap[64:128, :].rearrange("p (h q) -> p h q", h=4)
            op("dve", lambda e: e.tensor_tensor(out=outA.ap[0:64, 2 * kg:2 * kg + 2, tok0:tok0 + 128],
                                                in0=pO[:, 0:4:2, :], in1=rC[:, 0:4:2, :], op=ALU.mult),
               r=[pbO, r_], w=[outA])
            op("dve", lambda e: e.tensor_tensor(out=outA.ap[64:128, 2 * kg:2 * kg + 2, tok0:tok0 + 128],
                                                in0=pO[:, 1:4:2, :], in1=rC[:, 1:4:2, :], op=ALU.mult),
               r=[pbO, r_], w=[outA])

        bd_piece(100)
        swa_bias(0)
        for idx in range(len(its) + 2):
            if idx < len(its):
                swa_front(idx)
                a, kg = its[idx]
                if kg == 0 and a + 1 < GT:
                    swa_bias(a + 1)
            if 1 <= idx <= len(its):
                swa_PV(idx - 1)
            if idx >= 2:
                swa_back(idx - 2)
        if fr_next is not None:
            bd_backs(*fr_next)
    K.barrier()
    K.release(m0)

    lat = K.alloc([128, S], BF16, "lat")
    Kb2 = [K.alloc([128, S], BF16, "Kb") for _ in range(2)]
    m0 = K.mark()
    ctB = K.alloc([128, NB, 32], F32, "ctB")
    stB = K.alloc([128, NB, 32], F32, "stB")
    rope_tables_tokmajor(posbc, NB, ctB, stB)
    wlat = K.alloc([128, 8, 192], BF16, "wlat")
    mk_ = K.mark()
    stage = [K.alloc([128, 8, 256], F32, "stage") for _ in range(1)]
    load_w(wlat, w_in, 0, 8, 3072, 160, gain=g1c, chunk=256, stage=stage)
    op("dve", lambda e: e.tensor_scalar(out=wlat.ap[:, :, 160:176], in0=wlat.ap[:, :, 144:160], scalar1=-1.0,
                                        scalar2=None, op0=ALU.mult), r=[wlat], w=[wlat])
    op("dve", lambda e: e.tensor_copy(out=wlat.ap[:, :, 176:192], in_=wlat.ap[:, :, 128:144]), r=[wlat], w=[wlat])
    K.barrier()
    K.release(mk_)
    xts = [K.alloc([128, D], F32, "xt") for _ in range(4)]
    xss8 = [K.alloc([128, D], BF16, "xs") for _ in range(8)]
    stats = [K.alloc([128, 8], F32, "stat") for _ in range(2)]
    hTs = [K.alloc([128, 8, 512], BF16, "hT") for _ in range(2)]
    sq = K.alloc([128, 1, 512], F32, "sq")
    c32 = None
    ssb = K.alloc([128, 512], F32, "ssb")
    rr = K.alloc([128, 512], F32, "rr")
    cosg = K.alloc([128, 512], F32, "cosg")
    sing = K.alloc([128, 512], F32, "sing")
    ra = K.alloc([128, 512], F32, "ra")
    rb = K.alloc([128, 512], F32, "rb")
    def A_src(g):
        return [("dram", xb[(4 * g + a) * 128:(4 * g + a + 1) * 128, :]) for a in range(4)]

    def A_mm(g):
        hT = hTs[g % 2]
        for (pb, c0, M) in ((B[2], 0, 128), (B[3], 128, 64)):
            for k in range(8):
                op("pe", lambda e, k=k, pb=pb, c0=c0, M=M: e.matmul(pb.ap[0:M, :], lhsT=wlat.ap[:, k, c0:c0 + M],
                                                                   rhs=hT.ap[:, k, :], start=(k == 0), stop=(k == 7)),
                   r=[wlat, hT], w=[pb], inc=(k == 7))

    def A_post(g):
        rms_feature_major([B[2]], 128, 512, [(lat.ap[:, g * 512:(g + 1) * 512], lat)], sq, c32, ssb, rr, B[5])
        table_T(ctB, stB, 4 * g, 4, cosg, sing, B[6], B[7])
        op("dve", lambda e: e.tensor_tensor(out=ra.ap[0:32, :], in0=B[3].ap[0:32, :], in1=cosg.ap[0:32, :], op=ALU.mult),
           r=[B[3], cosg], w=[ra])
        op("dve", lambda e: e.tensor_tensor(out=rb.ap[0:32, :], in0=B[3].ap[32:64, :], in1=sing.ap[0:32, :], op=ALU.mult),
           r=[B[3], sing], w=[rb])
        for kb in Kb2:
            op("dve", lambda e, kb=kb: e.tensor_tensor(out=kb.ap[64:96, g * 512:(g + 1) * 512], in0=ra.ap[0:32, :],
                                                       in1=rb.ap[0:32, :], op=ALU.add), r=[ra, rb], w=[kb])

    nA = nt_front(A_src(0), xts, xss8[0:4], stats[0])
    nt_back(nA, hTs[0], xss8[0:4], [B[0], B[1]])
    for g in range(NLG):
        A_mm(g)
        if g + 1 < NLG:
            xs_n = xss8[4 * ((g + 1) % 2):4 * ((g + 1) % 2) + 4]
            nA = nt_front(A_src(g + 1), xts, xs_n, stats[(g + 1) % 2])
        if g + 1 < NLG:
            nt_back(nA, hTs[(g + 1) % 2], xs_n, [B[0], B[1]])
        A_post(g)
    K.barrier()
    K.release(m0)

    m0 = K.mark()
    wkvb = K.alloc([128, 1, 1024], BF16, "wkvb")
    wqb = K.alloc([128, 2, 768], BF16, "wqb")
    wqr = K.alloc([128, 2, 8, 32], BF16, "wqr")
    cosO = K.alloc([128, NT], F32, "cosO")
    sinO = K.alloc([128, NT], F32, "sinO")
    mk_ = K.mark()
    stage = [K.alloc([128, 2, 512], F32, "stage") for _ in range(2)]
    load_w(wkvb, w_kv_b, 0, 1, 0, 1024, gain=gkc, chunk=512, stage=stage)
    load_w(wqb, w_q_b, 0, 2, 0, 768, gain=gqc, chunk=384, stage=stage, extra_scale=SCALE_B)
    wq4 = wqb.ap.rearrange("p k (h c) -> p k h c", h=8)
    for k in range(2):
        op("dve", lambda e, k=k: e.tensor_scalar(out=wqr.ap[:, k, :, 0:16], in0=wq4[:, k, :, 80:96], scalar1=-1.0,
                                                 scalar2=None, op0=ALU.mult), r=[wqb], w=[wqr])
        op("dve", lambda e, k=k: e.tensor_copy(out=wqr.ap[:, k, :, 16:32], in_=wq4[:, k, :, 64:80]), r=[wqb], w=[wqr])
    for G in range(NG):
        cg = T(cosO.ap[:, G * GN:(G + 1) * GN], "cg")
        sg = T(sinO.ap[:, G * GN:(G + 1) * GN], "sg")
        cg.b = cosO.b
        sg.b = sinO.b
        table_T(ctO, stO, G * GT, GT, cg, sg, B[6], B[7])
    K.barrier()
    K.release(mk_)
    Vb2 = [K.alloc([128, NB, 128], BF16, "Vb") for _ in range(2)]
    QT2 = [K.alloc([128, NT], BF16, "QT") for _ in range(2)]
    qa2 = [K.alloc([128, GN], F32, "qa_") for _ in range(2)]
    qb2 = [K.alloc([128, GN], F32, "qb_") for _ in range(2)]
    NPT = 6
    PT = [K.alloc([128, 512], BF16, "PT") for _ in range(NPT)]
    rrB = [K.alloc([128, 512], F32, "rrB") for _ in range(2)]
    for vb in Vb2:
        op("pool", lambda e, vb=vb: e.memset(vb.ap[:, :, 64:128], 1.0), w=[vb])
    SB = [B[2], B[3], B[4], B[7]]
    OB = [B[5], B[6]]
    PB = [B[0], B[1]]
    pcnt = [0]

    def prod_chunks(h):
        kb = Kb2[h % 2]
        vb = Vb2[h % 2]
        qt = QT2[h % 2]
        ch = []
        for g in range(NLG):
            def f(g=g):
                pb = PB[pcnt[0] % 2]
                pcnt[0] += 1
                op("pe", lambda e: e.matmul(pb.ap[0:64, :], lhsT=wkvb.ap[:, 0, h * 128:h * 128 + 64],
                                            rhs=lat.ap[:, g * 512:(g + 1) * 512], start=True, stop=True),
                   r=[wkvb, lat], w=[pb])
                op("dve", lambda e: e.tensor_copy(out=kb.ap[0:64, g * 512:(g + 1) * 512], in_=pb.ap[0:64, :]),
                   r=[pb], w=[kb])
            ch.append(f)
        for g8 in range(0, NB, 8):
            def f(g8=g8):
                pb = PB[pcnt[0] % 2]
                pcnt[0] += 1
                nn = min(8, NB - g8)
                for i in range(nn):
                    n = g8 + i
                    op("pe", lambda e: e.matmul(pb.ap[:, i * 64:(i + 1) * 64], lhsT=lat.ap[:, n * 128:(n + 1) * 128],
                                                rhs=wkvb.ap[:, 0, h * 128 + 64:h * 128 + 128], start=True, stop=True),
                       r=[wkvb, lat], w=[pb], inc=(i == nn - 1))
                op("dve", lambda e: e.tensor_copy(out=vb.ap[:, g8:g8 + nn, 0:64],
                                                  in_=pb.ap[:, 0:nn * 64].rearrange("p (n d) -> p n d", d=64)),
                   r=[pb], w=[vb])
            ch.append(f)
        for G in range(NG):
            def f(G=G):
                tsl = slice(G * GN, (G + 1) * GN)
                qa_ = qa2[G % 2]
                qb_ = qb2[G % 2]
                for (pb, M, wsel) in ((B[0], 64, 0), (B[1], 32, 1), (B[1], 32, 2)):
                    for k in range(2):
                        if wsel == 0:
                            l_ap, rl = wqb.ap[:, k, h * 96:h * 96 + 64], wqb
                        elif wsel == 1:
                            l_ap, rl = wqb.ap[:, k, h * 96 + 64:h * 96 + 96], wqb
                        else:
                            l_ap, rl = wqr.ap[:, k, h, :], wqr
                        p0 = 64 if wsel == 2 else 0
                        op("pe", lambda e: e.matmul(pb.ap[p0:p0 + M, 0:GN], lhsT=l_ap, rhs=cqn.ap[:, k, tsl], start=(k == 0),
                                                    stop=(k == 1)), r=[rl, cqn], w=[pb], inc=(k == 1))
                op("dve", lambda e: e.tensor_copy(out=qt.ap[0:64, tsl], in_=B[0].ap[0:64, 0:GN]), r=[B[0]], w=[qt])
                op("dve", lambda e: e.tensor_tensor(out=qa_.ap[0:32, :], in0=B[1].ap[0:32, 0:GN], in1=cosO.ap[0:32, tsl],
                                                    op=ALU.mult), r=[B[1], cosO], w=[qa_])
                op("dve", lambda e: e.tensor_tensor(out=qb_.ap[0:32, :], in0=B[1].ap[64:96, 0:GN], in1=sinO.ap[0:32, tsl],
                                                    op=ALU.mult), r=[B[1], sinO], w=[qb_])
                op("dve", lambda e: e.tensor_tensor(out=qt.ap[64:96, tsl], in0=qa_.ap[0:32, :], in1=qb_.ap[0:32, :],
                                                    op=ALU.add), r=[qa_, qb_], w=[qt])
            ch.append(f)
        return ch

    for f in prod_chunks(0):
        f()
    og = 0
    units_per_head = sum(4 * (GT * G + GT - 1) + 4 for G in range(NG))
    for h in range(8):
        Kb = Kb2[h % 2]
        Vb = Vb2[h % 2]
        QT = QT2[h % 2]
        pend = prod_chunks(h + 1) if h + 1 < 8 else []
        every = max(1, units_per_head // (len(pend) + 1)) if pend else 0
        ucount = 0
        for G in range(NG):
            nlast = 4 * (GT * G + GT - 1) + 3
            pbO = OB[og % 2]
            rr_ = rrB[og % 2]
            og += 1
            steps = list(range(nlast + 1))

            def amin(n):
                return max(0, -((-(n - 3)) // 4) - GT * G)

            def emit_S(n):
                a0 = amin(n)
                pbS = SB[n % 4]
                c0 = a0 * 128
                masks = []
                for a in range(a0, GT):
                    i = n - 4 * (GT * G + a)
                    if 0 <= i <= 3:
                        masks.append((a, i))
                op("pe", lambda e: e.matmul(pbS.ap[:, c0:GN], lhsT=Kb.ap[0:96, n * 128:(n + 1) * 128],
                                            rhs=QT.ap[0:96, G * GN + c0:(G + 1) * GN], start=True, stop=(not masks)),
                   r=[Kb, QT], w=[pbS], inc=(not masks))
                for j, (a, i) in enumerate(masks):
                    last = (j == len(masks) - 1)
                    op("pe", lambda e, a=a, i=i, last=last: e.matmul(pbS.ap[:, a * 128:(a + 1) * 128], lhsT=ident_b.ap,
                                                                    rhs=mmask.ap[:, i, :], start=False, stop=last),
                       r=[ident_b, mmask], w=[pbS], inc=last)
                pt = PT[n % NPT]
                op("act", lambda e: e.activation(out=pt.ap[:, c0:GN], in_=pbS.ap[:, c0:GN], func=AF.Exp),
                   r=[pbS], w=[pt])

            def emit_PV(n):
                a0 = amin(n)
                c0 = a0 * 128
                pt = PT[n % NPT]
                op("pe", lambda e: e.matmul(pbO.ap[:, c0:GN], lhsT=Vb.ap[:, n, :], rhs=pt.ap[:, c0:GN],
                                            start=(n == 0), stop=(n == nlast)), r=[Vb, pt], w=[pbO], inc=True)

            LAG = 4
            for s_ in range(len(steps) + LAG):
                if s_ < len(steps):
                    emit_S(steps[s_])
                    ucount += 1
                    if pend and ucount % every == 0:
                        pend.pop(0)()
                if s_ >= LAG:
                    emit_PV(steps[s_ - LAG])
            hp = (h % 2) * 64
            op("dve", lambda e: e.reciprocal(out=rr_.ap[64:128, 0:GN], in_=pbO.ap[64:128, 0:GN]), r=[pbO], w=[rr_])
            op("dve", lambda e: e.tensor_tensor(out=outB.ap[hp:hp + 64, h // 2, G * GN:(G + 1) * GN],
                                                in0=pbO.ap[0:64, 0:GN], in1=rr_.ap[64:128, 0:GN], op=ALU.mult),
               r=[pbO, rr_], w=[outB])
        while pend:
            pend.pop(0)()
    K.barrier()
    K.release(MIX_MARK)
    K.top = CONST_MARK + 2 * ((4 * NT * 2 + 63) // 64 * 64)
    E_MARK = K.mark()

    hTa = K.alloc([128, 8, NT], BF16, "hTa")
    mT = K.alloc([128, 8, NT], BF16, "mT")
    stage = [K.alloc([128, 8, 256], F32, "stage") for _ in range(2)]
    sa = [K.alloc([128, 512], F32, "sa") for _ in range(2)]
    sb_ = [K.alloc([128, 512], F32, "sb") for _ in range(2)]
    t1 = [K.alloc([128, 512], F32, "t1") for _ in range(2)]
    t2 = [K.alloc([128, 512], F32, "t2") for _ in range(2)]

    def wset():
        return (K.alloc([128, 8, 512], BF16, "wga"), K.alloc([128, 8, 512], BF16, "wgb"),
                K.alloc([128, 4, 512], BF16, "woa"), K.alloc([128, 4, 512], BF16, "wob"))

    def wset_chunks(ws, half, ctr):
        ch = []
        ch += load_w(ws[0], w_in, 0, 8, half * 512, 512, gain=g1c, chunk=256, stage=stage, defer=True, ctr=ctr)
        ch += load_w(ws[1], w_in, 0, 8, 1024 + half * 512, 512, gain=g1c, chunk=256, stage=stage, defer=True, ctr=ctr)
        ch += load_w(ws[2], w_o_a, 0, 4, half * 512, 512, chunk=256, stage=stage, defer=True, ctr=ctr)
        ch += load_w(ws[3], w_o_b, 0, 4, half * 512, 512, chunk=256, stage=stage, defer=True, ctr=ctr)
        return ch

    sctr = [0]
    ws0 = wset()
    m1 = K.mark()
    xts = [K.alloc([128, D], F32, "xt") for _ in range(4)]
    xss = [K.alloc([128, D], BF16, "xs") for _ in range(4)]
    stat = K.alloc([128, 2 * GT], F32, "stat")
    pend0 = wset_chunks(ws0, 0, sctr)
    for G in range(NG):
        hv = T(hTa.ap[:, :, G * GN:(G + 1) * GN], "hv")
        hv.b = hTa.b
        n_ = nt_front([("dram", xo[(G * GT + a) * 128:(G * GT + a + 1) * 128, :]) for a in range(GT)], xts, xss, stat)
        for _ in range(2):
            if pend0:
                pend0.pop(0)()
        nt_back(n_, hv, xss, [B[0], B[1]])
    while pend0:
        pend0.pop(0)()
    K.barrier()
    K.release(m1)
    ws1 = wset()
    it = 0
    for half in range(2):
        wga, wgb, woa, wob = ws0 if half == 0 else ws1
        pend = wset_chunks(ws1, 1, sctr) if half == 0 else []
        for G in range(NG):
            tsl = slice(G * GN, (G + 1) * GN)
            for il in range(4):
                i = half * 4 + il
                bs = [B[0], B[1], B[2], B[3]] if it % 2 == 0 else [B[4], B[5], B[6], B[7]]
                u = it % 2
                it += 1
                csl = slice(il * 128, (il + 1) * 128)
                for (pb, wt, src, nk) in ((bs[0], wga, hTa, 8), (bs[1], wgb, hTa, 8), (bs[2], woa, outA, 4),
                                          (bs[3], wob, outB, 4)):
                    for k in range(nk):
                        op("pe", lambda e, k=k, pb=pb, wt=wt, src=src, nk=nk, csl=csl, tsl=tsl: e.matmul(
                            pb.ap[:, 0:GN], lhsT=wt.ap[:, k, csl], rhs=src.ap[:, k, tsl], start=(k == 0),
                            stop=(k == nk - 1)), r=[wt, src], w=[pb], inc=(k == nk - 1))
                op("act", lambda e, u=u, bs=bs: e.activation(out=sa[u].ap[:, 0:GN], in_=bs[0].ap[:, 0:GN],
                                                            func=AF.Sigmoid), r=[bs[0]], w=[sa[u]])
                op("act", lambda e, u=u, bs=bs: e.activation(out=sb_[u].ap[:, 0:GN], in_=bs[1].ap[:, 0:GN],
                                                            func=AF.Sigmoid), r=[bs[1]], w=[sb_[u]])
                op("dve", lambda e, u=u, bs=bs: e.tensor_tensor(out=t1[u].ap[:, 0:GN], in0=bs[2].ap[:, 0:GN],
                                                               in1=sa[u].ap[:, 0:GN], op=ALU.mult),
                   r=[bs[2], sa[u]], w=[t1[u]])
                op("dve", lambda e, u=u, bs=bs: e.tensor_tensor(out=t2[u].ap[:, 0:GN], in0=bs[3].ap[:, 0:GN],
                                                               in1=sb_[u].ap[:, 0:GN], op=ALU.mult),
                   r=[bs[3], sb_[u]], w=[t2[u]])
                op("dve", lambda e, u=u, i=i, tsl=tsl: e.tensor_tensor(out=mT.ap[:, i, tsl], in0=t1[u].ap[:, 0:GN],
                                                                       in1=t2[u].ap[:, 0:GN], op=ALU.add),
                   r=[t1[u], t2[u]], w=[mT])
                if pend:
                    pend.pop(0)()
        while pend:
            pend.pop(0)()
    K.barrier()

    K.top = E_MARK + 2 * 8 * NT * 2
    woh = [K.alloc([128, 8, 512], BF16, "wo") for _ in range(2)]
    stage = [K.alloc([128, 8, 256], F32, "stage") for _ in range(2)]
    for hf in range(2):
        load_w(woh[hf], w_out, 0, 8, hf * 512, 512, chunk=256, stage=stage)
    junk = K.alloc([128, 512], BF16, "junk")
    st2 = [K.alloc([128, 4], F32, "st2") for _ in range(4)]
    ytmp = [K.alloc([128, 512], F32, "ytmp") for _ in range(4)]
    xsl = [K.alloc([128, D], F32, "xsl") for _ in range(4)]

    def post_norm_residual(pbs, gb, res, u, junk, st2, ytmp):
        s = st2[u]
        for hf in range(2):
            op("act", lambda e, hf=hf: e.activation(out=junk.ap, in_=pbs[hf].ap, func=AF.Square,
                                                    accum_out=s.ap[:, hf:hf + 1]), r=[pbs[hf]], w=[junk, s])
        op("act", lambda e: e.activation(out=s.ap[:, 2:3], in_=s.ap[:, 0:1], func=AF.Identity, bias=s.ap[:, 1:2]),
           r=[s], w=[s])
        op("act", lambda e: e.activation(out=s.ap[:, 3:4], in_=s.ap[:, 2:3], func=AF.Ln, bias=EPS, scale=1.0 / D),
           r=[s], w=[s])
        op("act", lambda e: e.activation(out=s.ap[:, 3:4], in_=s.ap[:, 3:4], func=AF.Exp, scale=-0.5), r=[s], w=[s])
        for hf in range(2):
            yt = ytmp[(2 * u + hf) % len(ytmp)]
            op("dve", lambda e, hf=hf, yt=yt: e.scalar_tensor_tensor(
                out=yt.ap, in0=pbs[hf].ap, scalar=s.ap[:, 3:4], in1=gb.ap[:, hf * 512:(hf + 1) * 512], op0=ALU.mult,
                op1=ALU.mult), r=[pbs[hf], s, gb], w=[yt])
            op("dve", lambda e, hf=hf, yt=yt: e.tensor_tensor(out=res.ap[:, hf * 512:(hf + 1) * 512], in0=yt.ap,
                                                             in1=res.ap[:, hf * 512:(hf + 1) * 512], op=ALU.add),
               r=[yt, res], w=[res])

    for m in range(NQ):
        pbs = [B[2 * (m % 4)], B[2 * (m % 4) + 1]]
        xs_ = xsl[m % 4]
        K.dma("sp", xs_.ap, xo[m * 128:(m + 1) * 128, :], w=[xs_])
        for hf in range(2):
            for k in range(8):
                op("pe", lambda e, k=k, hf=hf, m=m: e.matmul(pbs[hf].ap, lhsT=mT.ap[:, k, m * 128:(m + 1) * 128],
                                                            rhs=woh[hf].ap[:, k, :], start=(k == 0),
                                                            stop=(k == 7)), r=[mT, woh[hf]], w=[pbs[hf]], inc=(k == 7))
        post_norm_residual(pbs, g2b, xs_, m % 4, junk, st2, ytmp)
        K.dma("act", out[m * 128:(m + 1) * 128, :], xs_.ap, r=[xs_])
    K.barrier()

    K.top = F_BASE
    TG = NT // 2 if NT >= 1024 else NT
    NP = NT // TG
    SN = min(512, TG)
    NS = TG // SN
    NTL = TG // 128
    aT = K.alloc([128, 32, TG], BF16, "aT")
    h2T = K.alloc([128, 8, TG], BF16, "h2T")
    wd = K.alloc([128, 32, D], BF16, "wd")
    WD_END = K.mark()
    K.top = WD_END - 32 * D * 2
    xts = [K.alloc([128, D], F32, "xt") for _ in range(8)]
    xss = [K.alloc([128, D], BF16, "xs") for _ in range(8)]
    statF = [K.alloc([128, 8], F32, "stat") for _ in range(2)]
    assert K.top <= WD_END
    K.top = WD_END
    UP_MARK = K.mark()
    wst = [K.alloc([128, 8, 256], F32, "wst") for _ in range(2)]
    wuc = [K.alloc([128, 8, 256], BF16, "wuc") for _ in range(2)]
    sqf = [K.alloc([128, 512], F32, "sqf") for _ in range(3)]
    dstg = [K.alloc([128, D], F32, "dstg") for _ in range(2)]
    K.top = UP_MARK
    junk = K.alloc([128, 512], BF16, "junk")
    st2 = [K.alloc([128, 4], F32, "st2") for _ in range(2)]
    ytmp = [K.alloc([128, 512], F32, "ytmp") for _ in range(4)]
    xsl = [K.alloc([128, D], F32, "xsl") for _ in range(3)]
    UB = [B[2], B[3], B[4], B[5], B[6], B[7]]
    NCH = 16
    for ps in range(NP):
        tiles = list(range(ps * NTL, (ps + 1) * NTL))
        fr = []
        for gi, c0 in enumerate(range(0, len(tiles), 4)):
            sub = tiles[c0:c0 + 4]
            hv = T(h2T.ap[:, :, c0 * 128:(c0 + len(sub)) * 128], "hv")
            hv.b = h2T.b
            xt_ = xts[4 * (gi % 2):4 * (gi % 2) + 4]
            xs_g = xss[4 * (gi % 2):4 * (gi % 2) + 4]
            n_ = nt_front([("dram", out[m * 128:(m + 1) * 128, :]) for m in sub], xt_, xs_g, statF[gi % 2])
            fr.append((n_, hv, xs_g))
            if len(fr) == 2:
                for (n2, hv2, xs2) in fr:
                    nt_back(n2, hv2, xs2, [B[0], B[1]])
                fr = []
        for (n2, hv2, xs2) in fr:
            nt_back(n2, hv2, xs2, [B[0], B[1]])
        K.barrier()

        def wup_dma(ch):
            st = wst[ch % 2]
            K.dma("sp", st.ap, w_up[:, ch * 256:(ch + 1) * 256].rearrange("(k p) c -> p k c", p=128), w=[st])

        def wup_cast(ch):
            wc = wuc[ch % 2]
            st = wst[ch % 2]
            op("dve", lambda e: e.tensor_tensor(out=wc.ap[:, 0:5, :], in0=st.ap[:, 0:5, :],
                                                in1=g3c.ap[:, 0:5, None].to_broadcast([128, 5, 256]), op=ALU.mult),
               r=[st, g3c], w=[wc])
            for k in range(5, 8):
                op("act", lambda e, k=k: e.activation(out=wc.ap[:, k, :], in_=st.ap[:, k, :], func=AF.Copy,
                                                      scale=g3c.ap[:, k:k + 1]), r=[st, g3c], w=[wc])

        def wd_dma(k):
            st = dstg[k % 2]
            K.dma("act", st.ap, w_down[k * 128:(k + 1) * 128, :], w=[st])

        def wd_cast(k):
            st = dstg[k % 2]
            if k % 2 == 0:
                op("dve", lambda e: e.tensor_copy(out=wd.ap[:, k, :], in_=st.ap), r=[st], w=[wd])
            else:
                op("act", lambda e: e.activation(out=wd.ap[:, k, :], in_=st.ap, func=AF.Copy), r=[st], w=[wd])

        wup_dma(0)
        wup_dma(1)
        wup_cast(0)
        wd_dma(0)
        wd_dma(1)
        ui = 0
        for ch in range(NCH):
            wc = wuc[ch % 2]
            if ch + 1 < NCH:
                wup_cast(ch + 1)
            if ch + 2 < NCH:
                wup_dma(ch + 2)
            for il in range(2):
                i = ch * 2 + il
                wd_cast(i)
                if i + 2 < 32:
                    wd_dma(i + 2)
                for s_ in range(NS):
                    pb = UB[ui % 6]
                    sf = sqf[ui % 3]
                    ui += 1
                    for k in range(8):
                        op("pe", lambda e, k=k, pb=pb, wc=wc, il=il, s_=s_: e.matmul(
                            pb.ap[:, 0:SN], lhsT=wc.ap[:, k, il * 128:(il + 1) * 128],
                            rhs=h2T.ap[:, k, s_ * SN:(s_ + 1) * SN], start=(k == 0), stop=(k == 7)),
                           r=[wc, h2T], w=[pb], inc=(k == 7))
                    op("act", lambda e, pb=pb, sf=sf: e.activation(out=sf.ap[:, 0:SN], in_=pb.ap[:, 0:SN], func=AF.Square),
                       r=[pb], w=[sf])
                    op("dve", lambda e, pb=pb, sf=sf, i=i, s_=s_: e.scalar_tensor_tensor(
                        out=aT.ap[:, i, s_ * SN:(s_ + 1) * SN], in0=pb.ap[:, 0:SN], scalar=0.0, in1=sf.ap[:, 0:SN],
                        op0=ALU.is_gt, op1=ALU.mult), r=[pb, sf], w=[aT])
        K.barrier()
        for ti, m in enumerate(tiles):
            pbs = [B[0], B[1]] if ti % 2 == 0 else [B[2], B[3]]
            xs_ = xsl[ti % 3]
            K.dma("sp", xs_.ap, out[m * 128:(m + 1) * 128, :], w=[xs_])
            for hf in range(2):
                for k in range(32):
                    op("pe", lambda e, k=k, hf=hf, ti=ti: e.matmul(pbs[hf].ap, lhsT=aT.ap[:, k, ti * 128:(ti + 1) * 128],
                                                                  rhs=wd.ap[:, k, hf * 512:(hf + 1) * 512],
                                                                  start=(k == 0), stop=(k == 31)),
                       r=[aT, wd], w=[pbs[hf]], inc=(k == 31))
            post_norm_residual(pbs, g4b, xs_, ti % 2, junk, st2, ytmp)
            K.dma("act", out[m * 128:(m + 1) * 128, :], xs_.ap, r=[xs_])
        K.barrier()
    print("kernel build: sems=%d" % K.nsem)
    K.finish()
    return nc


def host_inputs(S, x, positions, weights):
    NB = S // 128
    NQ = NB // 4
    NT = NQ * 128
    in_maps = []
    freqs = (10000.0 ** (-np.arange(0, 32, 2, dtype=np.float32) / 32)).astype(np.float32)
    freqrow = np.ascontiguousarray(np.broadcast_to(freqs[None, :], (128, 16))).astype(np.float32)
    slopes = (2.0 ** (-8.0 * np.arange(1, 9, dtype=np.float32) / 8)).astype(np.float32)
    nslope = np.ascontiguousarray(np.broadcast_to(-slopes[None, :], (128, 8))).astype(np.float32)
    kk = np.arange(128)[:, None]
    qq = np.arange(128)[None, :]
    tri = np.where(kk <= qq, 0.0, NEG).astype(np.float32)
    allneg = np.full((128, 128), NEG, np.float32)
    zero = np.zeros((128, 128), np.float32)
    sw_prev = np.where(kk > qq, 0.0, BIG).astype(np.float32)
    sw_cur = np.where(kk <= qq, 0.0, BIG).astype(np.float32)
    sw_all = np.full((128, 128), BIG, np.float32)
    for c in range(8):
        b, j = c // 4, c % 4
        own = [4 * m + j for m in range(NQ)]
        prev = [max(g - 1, 0) for g in own]
        xb = np.ascontiguousarray(x[b])
        xbl = xb.reshape(NB, 128, D)
        pos = positions[b].astype(np.int32)
        posl = pos.reshape(NB, 128)
        mm = np.stack([zero if i < j else (tri if i == j else allneg) for i in range(4)], axis=1)
        sm = np.stack([sw_all if j == 0 else sw_prev, sw_prev, sw_cur], axis=1)
        posk = np.empty((128, 2 * NQ), np.int32)
        for m in range(NQ):
            posk[:, 2 * m] = posl[prev[m]]
            posk[:, 2 * m + 1] = posl[own[m]]
        d = {
            "xb": xb,
            "xo": np.ascontiguousarray(xbl[own].reshape(NT, D)),
            "xp": np.ascontiguousarray(xbl[prev].reshape(NT, D)),
            "posbc": np.ascontiguousarray(posl.T),
            "posoc": np.ascontiguousarray(posl[own].T),
            "poso": np.ascontiguousarray(posl[own].reshape(NT)),
            "posk": posk,
            "mlamask": np.ascontiguousarray(mm),
            "swamask": np.ascontiguousarray(sm),
            "freqrow": freqrow,
            "nslope": nslope,
        }
        d.update(weights)
        in_maps.append(d)
    return in_maps


_NC_CACHE = {}


def run(S, x, positions, weights):
    NB = S // 128
    NQ = NB // 4
    if S not in _NC_CACHE:
        _NC_CACHE[S] = build(S)
    nc = _NC_CACHE[S]
    in_maps = host_inputs(S, x, positions, weights)
    res = run_bass_kernel_spmd(nc, in_maps, core_ids=list(range(8)))
    B_ = x.shape[0]
    outp = np.empty((B_, S, D), np.float32)
    ov = outp.reshape(B_, NB, 128, D)
    for c in range(8):
        b, j = c // 4, c % 4
        o = np.asarray(res.results[c]["out"]).reshape(NQ, 128, D)
        for m in range(NQ):
            ov[b, 4 * m + j] = o[m]
    return outp


def kernel(x, positions, pre_norm_mix, w_in, q_a_norm, w_q_b, kv_a_norm, w_kv_b, sinks, w_o_a, w_o_b, w_out,
           post_norm_mix, pre_norm_mlp, w_up, w_down, post_norm_mlp):
    f = lambda a: np.ascontiguousarray(np.asarray(a, dtype=np.float32)[0])
    weights = {
        "w_in": f(w_in), "w_q_b": f(w_q_b), "w_kv_b": f(w_kv_b), "w_o_a": f(w_o_a), "w_o_b": f(w_o_b),
        "w_out": f(w_out), "w_up": f(w_up), "w_down": f(w_down),
        "pre_norm_mix": f(pre_norm_mix), "q_a_norm": f(q_a_norm), "kv_a_norm": f(kv_a_norm), "sinks": f(sinks),
        "post_norm_mix": f(post_norm_mix), "pre_norm_mlp": f(pre_norm_mlp), "post_norm_mlp": f(post_norm_mlp),
    }
    x = np.asarray(x, dtype=np.float32)
    positions = np.asarray(positions)
    return run(x.shape[1], x, positions, weights)
```

```python
import contextlib
import math
import numpy as np
import concourse.bass as bass
import concourse.mybir as mybir
from concourse.bass_utils import run_bass_kernel_spmd

F32 = mybir.dt.float32
BF16 = mybir.dt.bfloat16
I32 = mybir.dt.int32
AF = mybir.ActivationFunctionType
ALU = mybir.AluOpType

D = 1024
DFF = 4096
EPS = 1e-6
NEG = -30000.0
BIG = 1.0e5
SCALE_A = 64 ** -0.5
SCALE_B = 96 ** -0.5
PI = math.pi
DSZ = {F32: 4, BF16: 2, I32: 4}
ARENA_BYTES = 195 * 1024
IARENA_ELEMS = 2048 + 1024 + 64 + 64
EPOCH = 30000


class Buf:
    __slots__ = ("name", "w", "r", "dsem", "dcnt")

    def __init__(self, name):
        self.name = name
        self.w = None
        self.r = {}
        self.dsem = None
        self.dcnt = 0


class T:
    def __init__(self, ap, name):
        self.ap = ap
        self.b = Buf(name)


class Eng:
    def __init__(self, kern, name, h, is_pe=False):
        self.kern = kern
        self.name = name
        self.h = h
        self.is_pe = is_pe
        self.sem = kern.new_sem(name)
        self.cnt = 0
        self.seen = {}
        self.pending = False

    def wait(self, tok):
        sem, val, _ = tok
        if self.seen.get(sem, 0) >= val:
            return
        self.h.wait_ge(sem, val)
        self.seen[sem] = val


class Kern:
    def __init__(self):
        self.nc = bass.Bass("TRN2", target_bir_lowering=False)
        self.es = contextlib.ExitStack()
        nc = self.nc
        self.nsem = 0
        self.arena = self.es.enter_context(nc.sbuf_tensor("arena", [128, ARENA_BYTES // 4], F32))
        self.top = 0
        self.iarena = self.es.enter_context(nc.sbuf_tensor("iarena", [128, IARENA_ELEMS], I32))
        self.banks = []
        for i in range(8):
            t = self.es.enter_context(nc.psum_tensor("bank%d" % i, [128, 512], F32))
            self.banks.append(T(t[:], "bank%d" % i))
        self.E = {
            "pe": Eng(self, "pe", nc.tensor, True),
            "act": Eng(self, "act", nc.scalar),
            "dve": Eng(self, "dve", nc.vector),
            "pool": Eng(self, "pool", nc.gpsimd),
            "sp": Eng(self, "sp", nc.sync),
        }
        self.dma_sems = {}
        self.sem_cnt = {}
        self.free_dsems = []
        self.dma_owners = []
        self.uid = 0

    def new_sem(self, name):
        self.nsem += 1
        return self.es.enter_context(self.nc.semaphore("s%d_%s" % (self.nsem, name)))

    def mark(self):
        return self.top

    def release(self, mark):
        self.top = mark

    def ialloc(self, shape, off, name):
        n = int(np.prod(shape[1:]))
        assert off + n <= IARENA_ELEMS
        ap = self.iarena[:, off:off + n]
        if len(shape) == 3:
            ap = ap.rearrange("p (a b) -> p a b", a=shape[1])
        self.uid += 1
        return T(ap, "%s_%d" % (name, self.uid))

    def alloc(self, shape, dt, name):
        assert dt != I32
        n = int(np.prod(shape[1:]))
        nbytes = (n * DSZ[dt] + 63) // 64 * 64
        off = self.top
        self.top += nbytes
        assert self.top <= ARENA_BYTES, "SBUF arena overflow at %s: %d" % (name, self.top)
        ap = self.arena[:, off // 4:(off + nbytes) // 4]
        if dt != F32:
            ap = ap.bitcast(dt)
        ap = ap[:, 0:n]
        if len(shape) == 3:
            ap = ap.rearrange("p (a b) -> p a b", a=shape[1])
        elif len(shape) == 4:
            ap = ap.rearrange("p (a b c) -> p a b c", a=shape[1], b=shape[2])
        elif len(shape) == 5:
            ap = ap.rearrange("p (a b c d) -> p a b c d", a=shape[1], b=shape[2], c=shape[3])
        self.uid += 1
        return T(ap, "%s_%d" % (name, self.uid))

    def _bufs(self, lst):
        return [x.b if isinstance(x, T) else x for x in lst]

    def op(self, en, fn, r=(), w=(), inc=True):
        eng = self.E[en]
        r = self._bufs(r)
        w = self._bufs(w)
        toks = []
        for b in r:
            if b.w is not None:
                toks.append(b.w)
        for b in w:
            if b.w is not None:
                toks.append(b.w)
            toks.extend(b.r.values())
        for tok in toks:
            if eng.is_pe and tok[2] is eng:
                continue
            eng.wait(tok)
        ins = fn(eng.h)
        if inc:
            if eng.cnt >= EPOCH:
                assert not eng.pending
                eng.sem = self.new_sem(eng.name)
                eng.cnt = 0
            eng.cnt += 1
            ins.then_inc(eng.sem, 1)
            eng.pending = False
            tok = (eng.sem, eng.cnt, eng)
        else:
            assert eng.cnt + 1 <= EPOCH
            eng.pending = True
            tok = (eng.sem, eng.cnt + 1, eng)
        for b in r:
            b.r[tok[0]] = tok
        for b in w:
            b.w = tok
            b.r = {}
        return ins

    def dma(self, q, out, in_, r=(), w=(), owner=None, **kw):
        eng = self.E[q]
        r = self._bufs(r)
        w = self._bufs(w)
        toks = []
        for b in r:
            if b.w is not None:
                toks.append(b.w)
        for b in w:
            if b.w is not None:
                toks.append(b.w)
            toks.extend(b.r.values())
        for tok in toks:
            eng.wait(tok)
        if owner is None:
            owner = w[0] if w else r[0]
        elif isinstance(owner, T):
            owner = owner.b
        if owner.dsem is None:
            if self.free_dsems:
                owner.dsem = self.free_dsems.pop()
            else:
                owner.dsem = self.new_sem("dma")
                self.sem_cnt[owner.dsem] = 0
            self.dma_owners.append(owner)
        self.sem_cnt[owner.dsem] += 16
        ins = eng.h.dma_start(out=out, in_=in_, **kw)
        ins.then_inc(owner.dsem, 16)
        tok = (owner.dsem, self.sem_cnt[owner.dsem], None)
        self.dma_sems[owner.dsem] = tok
        for b in r:
            b.r[tok[0]] = tok
        for b in w:
            b.w = tok
            b.r = {}
        return ins

    def barrier(self):
        sp = self.E["sp"]
        for e in self.E.values():
            assert not e.pending
            if e is not sp and e.cnt > 0:
                sp.wait((e.sem, e.cnt, e))
        for tok in self.dma_sems.values():
            sp.wait(tok)
        if sp.cnt >= EPOCH:
            sp.sem = self.new_sem("sp")
            sp.cnt = 0
        sp.cnt += 1
        sp.h.nop().then_inc(sp.sem, 1)
        btok = (sp.sem, sp.cnt, sp)
        allseen = dict(sp.seen)
        allseen[sp.sem] = sp.cnt
        for e in self.E.values():
            if e is not sp:
                e.wait(btok)
        for o in self.dma_owners:
            self.free_dsems.append(o.dsem)
            o.dsem = None
        self.dma_owners = []
        for e in self.E.values():
            for s, v in allseen.items():
                if e.seen.get(s, 0) < v:
                    e.seen[s] = v
            for e2 in self.E.values():
                if e.seen.get(e2.sem, 0) < e2.cnt:
                    e.seen[e2.sem] = e2.cnt

    def finish(self):
        self.barrier()
        self.es.close()


def build(S):
    NB = S // 128
    NQ = NB // 4
    NT = NQ * 128
    GT = min(4, NQ)
    NG = NQ // GT
    GN = GT * 128
    NLG = S // 512

    K = Kern()
    nc = K.nc
    op = K.op
    B = K.banks

    def din(name, shape, dt=F32):
        return nc.dram_tensor(name, list(shape), dt, kind="ExternalInput").ap()

    xb = din("xb", [S, D])
    xo = din("xo", [NT, D])
    xp = din("xp", [NT, D])
    posbc = din("posbc", [128, NB], I32)
    posoc = din("posoc", [128, NQ], I32)
    poso = din("poso", [NT], I32)
    posk = din("posk", [128, 2 * NQ], I32)
    mlamask = din("mlamask", [128, 4, 128])
    swamask = din("swamask", [128, 3, 128])
    freqrow = din("freqrow", [128, 16])
    nslope = din("nslope", [128, 8])
    w_in = din("w_in", [D, 3232])
    w_q_b = din("w_q_b", [256, 768])
    w_kv_b = din("w_kv_b", [128, 1024])
    w_o_a = din("w_o_a", [512, D])
    w_o_b = din("w_o_b", [512, D])
    w_out = din("w_out", [D, D])
    w_up = din("w_up", [D, DFF])
    w_down = din("w_down", [DFF, D])
    g_pre_mix = din("pre_norm_mix", [D])
    g_qa = din("q_a_norm", [256])
    g_kva = din("kv_a_norm", [128])
    sinks = din("sinks", [8])
    g_post_mix = din("post_norm_mix", [D])
    g_pre_mlp = din("pre_norm_mlp", [D])
    g_post_mlp = din("post_norm_mlp", [D])
    out = nc.dram_tensor("out", [NT, D], F32, kind="ExternalOutput").ap()

    ident_f = K.alloc([128, 128], F32, "ident_f")
    ident_b = K.alloc([128, 128], BF16, "ident_b")
    ones_f = K.alloc([128, 128], F32, "ones_f")
    g1c = K.alloc([128, 8], F32, "g1c")
    g3c = K.alloc([128, 8], F32, "g3c")
    gqc = K.alloc([128, 2], F32, "gqc")
    gkc = K.alloc([128, 1], F32, "gkc")
    g2b = K.alloc([128, D], F32, "g2b")
    g4b = K.alloc([128, D], F32, "g4b")
    F_BASE = K.mark()
    nsl = K.alloc([128, 8], F32, "nsl")
    frq = K.alloc([128, 16], F32, "frq")
    sinkrow = K.alloc([128, 8], F32, "sinkrow")
    sinkexp = K.alloc([128, 8, 128], F32, "sinkexp")
    mmask_f = K.alloc([128, 4, 128], F32, "mmask_f")
    mmask = K.alloc([128, 4, 128], BF16, "mmask")
    smask = K.alloc([128, 3, 128], F32, "smask")
    posk_i = K.ialloc([128, 2 * NQ], 2048 + 1024 + 64, "posk_i")
    nposk = K.alloc([128, 2 * NQ], F32, "nposk")
    CONST_MARK = K.mark()

    cl = [
        (g1c, g_pre_mix.rearrange("(k p) -> p k", p=128), True),
        (g3c, g_pre_mlp.rearrange("(k p) -> p k", p=128), True),
        (gqc, g_qa.rearrange("(k p) -> p k", p=128), True),
        (gkc, g_kva.rearrange("(k p) -> p k", p=128), True),
        (g2b, g_post_mix.partition_broadcast(128), False),
        (g4b, g_post_mlp.partition_broadcast(128), False),
        (nsl, nslope, False),
        (frq, freqrow, False),
        (sinkrow, sinks.partition_broadcast(128), False),
        (mmask_f, mlamask, False),
        (smask, swamask, False),
        (posk_i, posk, False),
    ]
    for t, src, slow in cl:
        if slow:
            K.dma("sp", t.ap, src, w=[t], allow_slow_non_contiguous=True)
        else:
            K.dma("sp", t.ap, src, w=[t])
    op("pool", lambda e: e.memset(ones_f.ap, 1.0), w=[ones_f])
    op("pool", lambda e: e.memset(ident_f.ap, 1.0), w=[ident_f])
    op("pool", lambda e: e.affine_select(out=ident_f.ap, in_=ident_f.ap, pattern=[[-1, 128]],
                                         compare_op=ALU.is_equal, fill=0.0, base=0, channel_multiplier=1),
       r=[ident_f], w=[ident_f])
    op("dve", lambda e: e.tensor_copy(out=ident_b.ap, in_=ident_f.ap), r=[ident_f], w=[ident_b])
    op("dve", lambda e: e.tensor_copy(out=mmask.ap, in_=mmask_f.ap), r=[mmask_f], w=[mmask])
    op("act", lambda e: e.activation(out=sinkrow.ap, in_=sinkrow.ap, func=AF.Exp), r=[sinkrow], w=[sinkrow])
    op("dve", lambda e: e.tensor_copy(out=sinkexp.ap, in_=sinkrow.ap[:, :, None].to_broadcast([128, 8, 128])),
       r=[sinkrow], w=[sinkexp])
    op("dve", lambda e: e.tensor_copy(out=nposk.ap, in_=posk_i.ap), r=[posk_i], w=[nposk])
    op("dve", lambda e: e.tensor_scalar(out=nposk.ap, in0=nposk.ap, scalar1=-1.0, scalar2=None, op0=ALU.mult),
       r=[nposk], w=[nposk])
    K.barrier()

    def load_w(dst, dram, r0, KT, c0, C, gain=None, chunk=256, stage=None, extra_scale=None, engs=("dve",),
               defer=False, ctr=None):
        if defer:
            out_ = []
            for cc0 in range(0, C, chunk):
                cc = min(chunk, C - cc0)

                def f(cc0=cc0, cc=cc):
                    st = stage[ctr[0] % len(stage)]
                    ctr[0] += 1
                    sub = T(dst.ap[:, :, cc0:cc0 + cc], "sub")
                    sub.b = dst.b
                    load_w(sub, dram, r0, KT, c0 + cc0, cc, gain=gain, chunk=cc, stage=[st], extra_scale=extra_scale)
                out_.append(f)
            return out_
        i = 0
        for cc0 in range(0, C, chunk):
            cc = min(chunk, C - cc0)
            st = stage[i % len(stage)]
            src = dram[r0:r0 + KT * 128, c0 + cc0:c0 + cc0 + cc].rearrange("(k p) c -> p k c", p=128)
            K.dma("sp", st.ap[:, 0:KT, 0:cc], src, w=[st])
            en = engs[i % len(engs)]
            o = dst.ap[:, 0:KT, cc0:cc0 + cc]
            s_ap = st.ap[:, 0:KT, 0:cc]
            if gain is None:
                if extra_scale is None:
                    op(en, lambda e, o=o, s_ap=s_ap: e.tensor_copy(out=o, in_=s_ap), r=[st], w=[dst])
                else:
                    op("dve", lambda e, o=o, s_ap=s_ap: e.tensor_scalar(out=o, in0=s_ap, scalar1=extra_scale, scalar2=None,
                                                                  op0=ALU.mult), r=[st], w=[dst])
            else:
                gb = gain.ap[:, 0:KT, None].to_broadcast([128, KT, cc])
                if extra_scale is None:
                    op(en, lambda e, o=o, s_ap=s_ap, gb=gb: e.tensor_tensor(out=o, in0=s_ap, in1=gb, op=ALU.mult),
                       r=[st, gain], w=[dst])
                else:
                    op("dve", lambda e, o=o, s_ap=s_ap, gb=gb: e.scalar_tensor_tensor(out=o, in0=s_ap, scalar=extra_scale,
                                                                                in1=gb, op0=ALU.mult, op1=ALU.mult),
                       r=[st, gain], w=[dst])
            i += 1

    def nt_front(srcs, xts, xss, stat, pieces=None, defer_dma=False):
        n = len(srcs)
        assert len(xss) >= n and (xts is None or len(xts) >= n)
        def emit(f):
            if pieces is None:
                f()
            else:
                pieces.append(f)

        xin = []
        for i, s in enumerate(srcs):
            if s[0] == "dram":
                xt = xts[i % len(xts)]
                if defer_dma and pieces is not None:
                    emit(lambda xt=xt, src=s[1]: K.dma("sp", xt.ap, src, w=[xt]))
                else:
                    K.dma("sp", xt.ap, s[1], w=[xt])
                xin.append((xt.ap, xt))
            else:
                xin.append((s[1], s[2]))

        for i in range(n):
            xs = xss[i % len(xss)]
            a, t = xin[i]
            emit(lambda a=a, xs=xs, i=i, t=t: op("act", lambda e: e.activation(
                out=xs.ap, in_=a, func=AF.Square, accum_out=stat.ap[:, i:i + 1]), r=[t], w=[xs, stat]))

        def rstd():
            op("act", lambda e: e.activation(out=stat.ap[:, n:2 * n], in_=stat.ap[:, 0:n], func=AF.Ln,
                                             bias=EPS, scale=1.0 / D), r=[stat], w=[stat])
            op("act", lambda e: e.activation(out=stat.ap[:, n:2 * n], in_=stat.ap[:, n:2 * n], func=AF.Exp, scale=-0.5),
               r=[stat], w=[stat])
        emit(rstd)
        for i in range(n):
            xs = xss[i % len(xss)]
            a, t = xin[i]
            emit(lambda a=a, xs=xs, i=i, t=t: op("dve", lambda e: e.tensor_scalar(
                out=xs.ap, in0=a, scalar1=stat.ap[:, n + i:n + i + 1], scalar2=None, op0=ALU.mult), r=[t, stat], w=[xs]))
        return n

    def nt_back(n, hT, xss, pbanks):
        for i in range(n):
            xs = xss[i % len(xss)]
            pb = pbanks[i % len(pbanks)]
            pbv = pb.ap.bitcast(BF16)
            for k in range(8):
                op("pe", lambda e, k=k, xs=xs, pbv=pbv: e.transpose(out=pbv[:, k * 128:(k + 1) * 128],
                                                                   in_=xs.ap[:, k * 128:(k + 1) * 128],
                                                                   identity=ident_b.ap),
                   r=[xs, ident_b], w=[pb], inc=(k == 7))
            o_ap = hT.ap[:, :, i * 128:(i + 1) * 128]
            i_ap = pbv.rearrange("p (k n) -> p k n", k=8)
            if i % 4 == 0:
                op("act", lambda e, o_ap=o_ap, i_ap=i_ap: e.activation(out=o_ap, in_=i_ap, func=AF.Copy), r=[pb], w=[hT])
            else:
                op("dve", lambda e, o_ap=o_ap, i_ap=i_ap: e.tensor_copy(out=o_ap, in_=i_ap), r=[pb], w=[hT])

    def norm_transpose(srcs, hT, xts, xss, pbanks, stat):
        n = nt_front(srcs, xts, xss, stat)
        nt_back(n, hT, xss, pbanks)

    def rope_tables_tokmajor(pos_cols_dram, ntile, ct, st_):
        m0 = K.mark()
        pi_ = K.ialloc([128, ntile], 2048 + 1024, "pos_i")
        pf = K.alloc([128, ntile], F32, "pos_f")
        ang = K.alloc([128, ntile, 16], F32, "ang")
        a2 = K.alloc([128, ntile, 16], F32, "a2")
        u = K.alloc([128, ntile, 16], F32, "u")
        ki = K.ialloc([128, ntile, 16], 2048, "ki")
        kf = K.alloc([128, ntile, 16], F32, "kf")
        K.dma("sp", pi_.ap, pos_cols_dram, w=[pi_])
        op("dve", lambda e: e.tensor_copy(out=pf.ap, in_=pi_.ap), r=[pi_], w=[pf])
        op("dve", lambda e: e.tensor_tensor(out=ang.ap, in0=pf.ap[:, :, None].to_broadcast([128, ntile, 16]),
                                            in1=frq.ap[:, None, :].to_broadcast([128, ntile, 16]), op=ALU.mult),
           r=[pf, frq], w=[ang])
        for dst, shift in ((st_, 0.0), (ct, PI / 2)):
            op("dve", lambda e: e.tensor_scalar(out=a2.ap, in0=ang.ap, scalar1=shift, scalar2=None, op0=ALU.add),
               r=[ang], w=[a2])
            op("dve", lambda e: e.tensor_scalar(out=u.ap, in0=a2.ap, scalar1=1.0 / (2 * PI), scalar2=None, op0=ALU.mult),
               r=[a2], w=[u])
            op("dve", lambda e: e.tensor_copy(out=ki.ap, in_=u.ap), r=[u], w=[ki])
            op("dve", lambda e: e.tensor_copy(out=kf.ap, in_=ki.ap), r=[ki], w=[kf])
            op("dve", lambda e: e.scalar_tensor_tensor(out=a2.ap, in0=kf.ap, scalar=-2 * PI, in1=a2.ap, op0=ALU.mult,
                                                       op1=ALU.add), r=[kf, a2], w=[a2])
            op("dve", lambda e: e.tensor_scalar(out=u.ap, in0=a2.ap, scalar1=PI, scalar2=-2 * PI, op0=ALU.is_gt,
                                                op1=ALU.mult), r=[a2], w=[u])
            op("dve", lambda e: e.tensor_tensor(out=a2.ap, in0=a2.ap, in1=u.ap, op=ALU.add), r=[a2, u], w=[a2])
            op("dve", lambda e: e.tensor_scalar(out=u.ap, in0=a2.ap, scalar1=-PI, scalar2=2 * PI, op0=ALU.is_lt,
                                                op1=ALU.mult), r=[a2], w=[u])
            op("dve", lambda e: e.tensor_tensor(out=a2.ap, in0=a2.ap, in1=u.ap, op=ALU.add), r=[a2, u], w=[a2])
            op("dve", lambda e: e.tensor_scalar(out=a2.ap, in0=a2.ap, scalar1=PI, scalar2=-PI, op0=ALU.min, op1=ALU.max),
               r=[a2], w=[a2])
            op("act", lambda e, dst=dst: e.activation(out=dst.ap[:, :, 0:16], in_=a2.ap, func=AF.Sin), r=[a2], w=[dst])
            op("dve", lambda e, dst=dst: e.tensor_copy(out=dst.ap[:, :, 16:32], in_=dst.ap[:, :, 0:16]), r=[dst], w=[dst])
        K.barrier()
        K.release(m0)

    def table_T(ct, st_, t0, nt, cosd, sind, pb_c, pb_s):
        for (src, dst, pb) in ((ct, cosd, pb_c), (st_, sind, pb_s)):
            for i in range(nt):
                op("pe", lambda e, src=src, pb=pb, i=i: e.matmul(pb.ap[0:32, i * 128:(i + 1) * 128],
                                                                lhsT=src.ap[:, t0 + i, :], rhs=ident_f.ap,
                                                                start=True, stop=True),
                   r=[src, ident_f], w=[pb], inc=(i == nt - 1))
            op("act", lambda e, dst=dst, pb=pb: e.activation(out=dst.ap[0:32, 0:nt * 128], in_=pb.ap[0:32, 0:nt * 128],
                                                            func=AF.Copy), r=[pb], w=[dst])

    def rms_feature_major(ps_list, nfeat, N, dst_list, sq, c32, ssb, rr, pb_ss):
        n = len(ps_list)
        for i, ps in enumerate(ps_list):
            op("act", lambda e, ps=ps, i=i: e.activation(out=sq.ap[:, i, 0:N], in_=ps.ap[:, 0:N], func=AF.Square),
               r=[ps], w=[sq])
        for i in range(n):
            op("pe", lambda e, i=i: e.matmul(pb_ss.ap[:, 0:N], lhsT=ones_f.ap, rhs=sq.ap[:, i, 0:N], start=(i == 0),
                                             stop=(i == n - 1)), r=[ones_f, sq], w=[pb_ss], inc=(i == n - 1))
        op("act", lambda e: e.activation(out=ssb.ap[:, 0:N], in_=pb_ss.ap[:, 0:N], func=AF.Ln, bias=EPS,
                                         scale=1.0 / nfeat), r=[pb_ss], w=[ssb])
        op("act", lambda e: e.activation(out=rr.ap[:, 0:N], in_=ssb.ap[:, 0:N], func=AF.Exp, scale=-0.5),
           r=[ssb], w=[rr])
        for i in range(n):
            d_ap, d_t = dst_list[i]
            ps = ps_list[i]
            op("dve", lambda e, i=i, d_ap=d_ap, ps=ps: e.tensor_tensor(out=d_ap, in0=ps.ap[:, 0:N], in1=rr.ap[:, 0:N],
                                                                      op=ALU.mult), r=[ps, rr], w=[d_t])

    outA = K.alloc([128, 4, NT], BF16, "outA")
    outB = K.alloc([128, 4, NT], BF16, "outB")
    cqn = K.alloc([128, 2, NT], BF16, "cqn")
    ctO = K.alloc([128, NQ, 32], F32, "ctO")
    stO = K.alloc([128, NQ, 32], F32, "stO")
    MIX_MARK = K.mark()

    rope_tables_tokmajor(posoc, NQ, ctO, stO)

    m0 = K.mark()
    wproj = K.alloc([128, 8, 1024], BF16, "wproj")
    posq = K.alloc([128, NT], F32, "posq")
    mk_ = K.mark()
    stage = [K.alloc([128, 8, 256], F32, "stage") for _ in range(2)]
    posq_i = K.ialloc([128, NT], 0, "posq_i")
    load_w(wproj, w_in, 0, 8, 2048, 1024, gain=g1c, chunk=256, stage=stage)
    K.dma("sp", posq_i.ap, poso.partition_broadcast(128), w=[posq_i])
    op("dve", lambda e: e.tensor_copy(out=posq.ap, in_=posq_i.ap), r=[posq_i], w=[posq])
    K.barrier()
    K.release(mk_)
    QA0, KA0, VA0, CQ0 = 0, 512, 640, 768
    xts = [K.alloc([128, D], F32, "xt") for _ in range(4)]
    xss = [K.alloc([128, D], BF16, "xs") for _ in range(8)]
    statO = K.alloc([128, 2 * GT], F32, "statO")
    statP = K.alloc([128, 2 * GT], F32, "statP")
    hTo = K.alloc([128, 8, GN], BF16, "hTo")
    hTp = K.alloc([128, 8, GN], BF16, "hTp")
    sq = K.alloc([128, 2, GN], F32, "sq")
    c32 = None
    ssb = K.alloc([128, GN], F32, "ssb")
    rr = K.alloc([128, GN], F32, "rr")
    QaT = K.alloc([128, 8, GN], BF16, "QaT")
    kaT = K.alloc([128, 2, 2, GN], BF16, "kaT")
    vaug = K.alloc([128, 2 * GT * 2, 128], BF16, "vaug")
    dist2 = [K.alloc([128, 2, 128], F32, "dist") for _ in range(2)]
    bias2 = [K.alloc([128, 2, 8, 128], F32, "biasall") for _ in range(2)]
    tS = [K.alloc([128, 512], F32, "tS") for _ in range(4)]
    PTa = [K.alloc([128, 512], BF16, "PTa") for _ in range(4)]
    rrA = [K.alloc([128, 512], F32, "rrA") for _ in range(2)]
    op("pool", lambda e: e.memset(vaug.ap[:, :, 64:128], 1.0), w=[vaug])
    SBA = [B[0], B[1], B[2], B[3]]
    OBA = [B[4], B[5]]

    xtsP = xts
    bd_pieces = []

    def bd_fronts(G, pieces=None):
        t0_ = G * GT
        nO = nt_front([("dram", xo[(t0_ + a) * 128:(t0_ + a + 1) * 128, :]) for a in range(GT)], xts, xss[0:4], statO,
                      pieces)
        nP = nt_front([("dram", xp[(t0_ + a) * 128:(t0_ + a + 1) * 128, :]) for a in range(GT)], xtsP, xss[4:8], statP,
                      pieces, defer_dma=True)
        return nO, nP

    def bd_piece(k=1):
        for _ in range(k):
            if bd_pieces:
                bd_pieces.pop(0)()

    def bd_backs(nO, nP):
        nt_back(nO, hTo, xss[0:4], [B[6], B[7]])
        nt_back(nP, hTp, xss[4:8], [B[6], B[7]])

    bd_backs(*bd_fronts(0))
    for G in range(NG):
        t0 = G * GT
        fr_next = bd_fronts(G + 1, bd_pieces) if G + 1 < NG else None
        for mt in range(2):
            pb = B[2 + mt]
            for k in range(8):
                op("pe", lambda e, k=k, mt=mt, pb=pb: e.matmul(pb.ap[:, 0:GN],
                                                              lhsT=wproj.ap[:, k, CQ0 + mt * 128:CQ0 + (mt + 1) * 128],
                                                              rhs=hTo.ap[:, k, :], start=(k == 0), stop=(k == 7)),
                   r=[wproj, hTo], w=[pb], inc=(k == 7))
        rms_feature_major([B[2], B[3]], 256, GN, [(cqn.ap[:, mt, G * GN:(G + 1) * GN], cqn) for mt in range(2)],
                          sq, c32, ssb, rr, B[4])
        bd_piece(2)
        for h in range(8):
            pb = B[h % 4]
            for k in range(8):
                op("pe", lambda e, k=k, h=h, pb=pb: e.matmul(pb.ap[0:64, 0:GN],
                                                            lhsT=wproj.ap[:, k, QA0 + h * 64:QA0 + (h + 1) * 64],
                                                            rhs=hTo.ap[:, k, :], start=(k == 0), stop=(k == 7)),
                   r=[wproj, hTo], w=[pb], inc=(k == 7))
            if h % 2 == 0:
                op("act", lambda e, h=h, pb=pb: e.activation(out=QaT.ap[0:64, h, :], in_=pb.ap[0:64, 0:GN], func=AF.Copy,
                                                            scale=SCALE_A), r=[pb], w=[QaT])
            else:
                op("dve", lambda e, h=h, pb=pb: e.tensor_scalar(out=QaT.ap[0:64, h, :], in0=pb.ap[0:64, 0:GN],
                                                               scalar1=SCALE_A, scalar2=None, op0=ALU.mult),
                   r=[pb], w=[QaT])
            bd_piece(1)
        i = 0
        for kg in range(2):
            for wch, hT in ((0, hTp), (1, hTo)):
                pb = B[i % 4]
                for k in range(8):
                    op("pe", lambda e, k=k, kg=kg, hT=hT, pb=pb: e.matmul(
                        pb.ap[0:64, 0:GN], lhsT=wproj.ap[:, k, KA0 + kg * 64:KA0 + (kg + 1) * 64], rhs=hT.ap[:, k, :],
                        start=(k == 0), stop=(k == 7)), r=[wproj, hT], w=[pb], inc=(k == 7))
                if i % 2 == 0:
                    op("act", lambda e, kg=kg, wch=wch, pb=pb: e.activation(out=kaT.ap[0:64, kg, wch, :],
                                                                          in_=pb.ap[0:64, 0:GN], func=AF.Copy),
                       r=[pb], w=[kaT])
                else:
                    op("dve", lambda e, kg=kg, wch=wch, pb=pb: e.tensor_copy(out=kaT.ap[0:64, kg, wch, :],
                                                                           in_=pb.ap[0:64, 0:GN]), r=[pb], w=[kaT])
                i += 1
                bd_piece(2)
        for wch, hT in ((0, hTp), (1, hTo)):
            pb = B[4 + wch]
            for a in range(GT):
                for k in range(8):
                    op("pe", lambda e, k=k, a=a, hT=hT, pb=pb: e.matmul(
                        pb.ap[:, a * 128:(a + 1) * 128], lhsT=hT.ap[:, k, a * 128:(a + 1) * 128],
                        rhs=wproj.ap[:, k, VA0:VA0 + 128], start=(k == 0), stop=(k == 7)), r=[wproj, hT], w=[pb],
                       inc=(k == 7 and a == GT - 1))
            o_ap = vaug.ap[:, wch * GT * 2:(wch + 1) * GT * 2, 0:64]
            i_ap = pb.ap[:, 0:GT * 128].rearrange("p (g d) -> p g d", d=64)
            if wch == 0:
                op("act", lambda e, o_ap=o_ap, i_ap=i_ap: e.activation(out=o_ap, in_=i_ap, func=AF.Copy), r=[pb], w=[vaug])
            else:
                op("dve", lambda e, o_ap=o_ap, i_ap=i_ap: e.tensor_copy(out=o_ap, in_=i_ap), r=[pb], w=[vaug])

        its = [(a, kg) for a in range(GT) for kg in range(2)]

        def swa_bias(a):
            m = t0 + a
            ba = bias2[a % 2]
            di = dist2[a % 2]
            for wch in range(2):
                op("act", lambda e: e.activation(out=di.ap[:, wch, :], in_=posq.ap[:, m * 128:(m + 1) * 128],
                                                 func=AF.Abs, bias=nposk.ap[:, 2 * m + wch:2 * m + wch + 1]),
                   r=[posq, nposk], w=[di])
                mi = (0 if m == 0 else 1) if wch == 0 else 2
                op("dve", lambda e: e.tensor_tensor(out=di.ap[:, wch, :], in0=di.ap[:, wch, :], in1=smask.ap[:, mi, :],
                                                    op=ALU.add), r=[di, smask], w=[di])
                op("dve", lambda e: e.tensor_tensor(
                    out=ba.ap[:, wch, :, :], in0=di.ap[:, wch, :][:, None, :].to_broadcast([128, 8, 128]),
                    in1=nsl.ap[:, :, None].to_broadcast([128, 8, 128]), op=ALU.mult), r=[di, nsl], w=[ba])

        def swa_front(idx):
            a, kg = its[idx]
            ba = bias2[a % 2]
            u = idx % 2
            pbO = OBA[u]
            for wch in range(2):
                pbS = SBA[2 * u + wch]
                ts_ = tS[2 * u + wch]
                pt = PTa[2 * u + wch]
                for hh in range(4):
                    op("pe", lambda e, hh=hh: e.matmul(
                        pbS.ap[:, hh * 128:(hh + 1) * 128], lhsT=kaT.ap[0:64, kg, wch, a * 128:(a + 1) * 128],
                        rhs=QaT.ap[0:64, 4 * kg + hh, a * 128:(a + 1) * 128], start=True, stop=True),
                       r=[kaT, QaT], w=[pbS], inc=(hh == 3))
                op("dve", lambda e: e.tensor_tensor(
                    out=ts_.ap, in0=pbS.ap, in1=ba.ap[:, wch, 4 * kg:4 * kg + 4, :].rearrange("p h q -> p (h q)"),
                    op=ALU.add), r=[pbS, ba], w=[ts_])
                op("act", lambda e: e.activation(out=pt.ap, in_=ts_.ap, func=AF.Exp), r=[ts_], w=[pt])

        def swa_PV(idx):
            a, kg = its[idx]
            u = idx % 2
            pbO = OBA[u]
            for wch in range(2):
                pt = PTa[2 * u + wch]
                op("pe", lambda e: e.matmul(pbO.ap, lhsT=vaug.ap[:, (wch * GT + a) * 2 + kg, :], rhs=pt.ap,
                                            start=(wch == 0), stop=(wch == 1)), r=[vaug, pt], w=[pbO], inc=(wch == 1))

        def swa_back(idx):
            a, kg = its[idx]
            u = idx % 2
            pbO = OBA[u]
            r_ = rrA[u]
            tok0 = (t0 + a) * 128
            for hh in range(4):
                op("act", lambda e, hh=hh: e.activation(
                    out=r_.ap[64:128, hh * 128:(hh + 1) * 128], in_=pbO.ap[64:128, hh * 128:(hh + 1) * 128], func=AF.Ln,
                    bias=sinkexp.ap[64:128, 4 * kg + hh, 0:1]), r=[pbO, sinkexp], w=[r_])
            op("act", lambda e: e.activation(out=r_.ap[64:128, :], in_=r_.ap[64:128, :], func=AF.Exp, scale=-1.0),
               r=[r_], w=[r_])
            pO = pbO.ap[0:64, :].rearrange("p (h q) -> p h q", h=4)
            rC = r_.ap[64:128, :].rearrange("p (h q) -> p h q", h=4)
            op("dve", lambda e: e.tensor_tensor(out=outA.ap[0:64, 2 * kg:2 * kg + 2, tok0:tok0 + 128],
                                                in0=pO[:, 0:4:2, :], in1=rC[:, 0:4:2, :], op=ALU.mult),
               r=[pbO, r_], w=[outA])
            op("dve", lambda e: e.tensor_tensor(out=outA.ap[64:128, 2 * kg:2 * kg + 2, tok0:tok0 + 128],
                                                in0=pO[:, 1:4:2, :], in1=rC[:, 1:4:2, :], op=ALU.mult),
               r=[pbO, r_], w=[outA])

        bd_piece(100)
        swa_bias(0)
        for idx in range(len(its) + 2):
            if idx < len(its):
                swa_front(idx)
                a, kg = its[idx]
                if kg == 0 and a + 1 < GT:
                    swa_bias(a + 1)
            if 1 <= idx <= len(its):
                swa_PV(idx - 1)
            if idx >= 2:
                swa_back(idx - 2)
        if fr_next is not None:
            bd_backs(*fr_next)
    K.barrier()
    K.release(m0)

    lat = K.alloc([128, S], BF16, "lat")
    Kb2 = [K.alloc([128, S], BF16, "Kb") for _ in range(2)]
    m0 = K.mark()
    ctB = K.alloc([128, NB, 32], F32, "ctB")
    stB = K.alloc([128, NB, 32], F32, "stB")
    rope_tables_tokmajor(posbc, NB, ctB, stB)
    wlat = K.alloc([128, 8, 192], BF16, "wlat")
    mk_ = K.mark()
    stage = [K.alloc([128, 8, 256], F32, "stage") for _ in range(1)]
    load_w(wlat, w_in, 0, 8, 3072, 160, gain=g1c, chunk=256, stage=stage)
    op("dve", lambda e: e.tensor_scalar(out=wlat.ap[:, :, 160:176], in0=wlat.ap[:, :, 144:160], scalar1=-1.0,
                                        scalar2=None, op0=ALU.mult), r=[wlat], w=[wlat])
    op("dve", lambda e: e.tensor_copy(out=wlat.ap[:, :, 176:192], in_=wlat.ap[:, :, 128:144]), r=[wlat], w=[wlat])
    K.barrier()
    K.release(mk_)
    xts = [K.alloc([128, D], F32, "xt") for _ in range(4)]
    xss8 = [K.alloc([128, D], BF16, "xs") for _ in range(8)]
    stats = [K.alloc([128, 8], F32, "stat") for _ in range(2)]
    hTs = [K.alloc([128, 8, 512], BF16, "hT") for _ in range(2)]
    sq = K.alloc([128, 1, 512], F32, "sq")
    c32 = None
    ssb = K.alloc([128, 512], F32, "ssb")
    rr = K.alloc([128, 512], F32, "rr")
    cosg = K.alloc([128, 512], F32, "cosg")
    sing = K.alloc([128, 512], F32, "sing")
    ra = K.alloc([128, 512], F32, "ra")
    rb = K.alloc([128, 512], F32, "rb")
    def A_src(g):
        return [("dram", xb[(4 * g + a) * 128:(4 * g + a + 1) * 128, :]) for a in range(4)]

    def A_mm(g):
        hT = hTs[g % 2]
        for (pb, c0, M) in ((B[2], 0, 128), (B[3], 128, 64)):
            for k in range(8):
                op("pe", lambda e, k=k, pb=pb, c0=c0, M=M: e.matmul(pb.ap[0:M, :], lhsT=wlat.ap[:, k, c0:c0 + M],
                                                                   rhs=hT.ap[:, k, :], start=(k == 0), stop=(k == 7)),
                   r=[wlat, hT], w=[pb], inc=(k == 7))

    def A_post(g):
        rms_feature_major([B[2]], 128, 512, [(lat.ap[:, g * 512:(g + 1) * 512], lat)], sq, c32, ssb, rr, B[5])
        table_T(ctB, stB, 4 * g, 4, cosg, sing, B[6], B[7])
        op("dve", lambda e: e.tensor_tensor(out=ra.ap[0:32, :], in0=B[3].ap[0:32, :], in1=cosg.ap[0:32, :], op=ALU.mult),
           r=[B[3], cosg], w=[ra])
        op("dve", lambda e: e.tensor_tensor(out=rb.ap[0:32, :], in0=B[3].ap[32:64, :], in1=sing.ap[0:32, :], op=ALU.mult),
           r=[B[3], sing], w=[rb])
        for kb in Kb2:
            op("dve", lambda e, kb=kb: e.tensor_tensor(out=kb.ap[64:96, g * 512:(g + 1) * 512], in0=ra.ap[0:32, :],
                                                       in1=rb.ap[0:32, :], op=ALU.add), r=[ra, rb], w=[kb])

    nA = nt_front(A_src(0), xts, xss8[0:4], stats[0])
    nt_back(nA, hTs[0], xss8[0:4], [B[0], B[1]])
    for g in range(NLG):
        A_mm(g)
        if g + 1 < NLG:
            xs_n = xss8[4 * ((g + 1) % 2):4 * ((g + 1) % 2) + 4]
            nA = nt_front(A_src(g + 1), xts, xs_n, stats[(g + 1) % 2])
        A_post(g)
        if g + 1 < NLG:
            nt_back(nA, hTs[(g + 1) % 2], xs_n, [B[0], B[1]])
    K.barrier()
    K.release(m0)

    m0 = K.mark()
    wkvb = K.alloc([128, 1, 1024], BF16, "wkvb")
    wqb = K.alloc([128, 2, 768], BF16, "wqb")
    wqr = K.alloc([128, 2, 8, 32], BF16, "wqr")
    cosO = K.alloc([128, NT], F32, "cosO")
    sinO = K.alloc([128, NT], F32, "sinO")
    mk_ = K.mark()
    stage = [K.alloc([128, 2, 512], F32, "stage") for _ in range(2)]
    load_w(wkvb, w_kv_b, 0, 1, 0, 1024, gain=gkc, chunk=512, stage=stage)
    load_w(wqb, w_q_b, 0, 2, 0, 768, gain=gqc, chunk=384, stage=stage, extra_scale=SCALE_B)
    wq4 = wqb.ap.rearrange("p k (h c) -> p k h c", h=8)
    for k in range(2):
        op("dve", lambda e, k=k: e.tensor_scalar(out=wqr.ap[:, k, :, 0:16], in0=wq4[:, k, :, 80:96], scalar1=-1.0,
                                                 scalar2=None, op0=ALU.mult), r=[wqb], w=[wqr])
        op("dve", lambda e, k=k: e.tensor_copy(out=wqr.ap[:, k, :, 16:32], in_=wq4[:, k, :, 64:80]), r=[wqb], w=[wqr])
    for G in range(NG):
        cg = T(cosO.ap[:, G * GN:(G + 1) * GN], "cg")
        sg = T(sinO.ap[:, G * GN:(G + 1) * GN], "sg")
        cg.b = cosO.b
        sg.b = sinO.b
        table_T(ctO, stO, G * GT, GT, cg, sg, B[6], B[7])
    K.barrier()
    K.release(mk_)
    Vb2 = [K.alloc([128, NB, 128], BF16, "Vb") for _ in range(2)]
    QT2 = [K.alloc([128, NT], BF16, "QT") for _ in range(2)]
    qa2 = [K.alloc([128, GN], F32, "qa_") for _ in range(2)]
    qb2 = [K.alloc([128, GN], F32, "qb_") for _ in range(2)]
    NPT = 6
    PT = [K.alloc([128, 512], BF16, "PT") for _ in range(NPT)]
    rrB = [K.alloc([128, 512], F32, "rrB") for _ in range(2)]
    for vb in Vb2:
        op("pool", lambda e, vb=vb: e.memset(vb.ap[:, :, 64:128], 1.0), w=[vb])
    SB = [B[2], B[3], B[4], B[7]]
    OB = [B[5], B[6]]
    PB = [B[0], B[1]]
    pcnt = [0]

    def prod_chunks(h):
        kb = Kb2[h % 2]
        vb = Vb2[h % 2]
        qt = QT2[h % 2]
        ch = []
        for g in range(NLG):
            def f(g=g):
                pb = PB[pcnt[0] % 2]
                pcnt[0] += 1
                op("pe", lambda e: e.matmul(pb.ap[0:64, :], lhsT=wkvb.ap[:, 0, h * 128:h * 128 + 64],
                                            rhs=lat.ap[:, g * 512:(g + 1) * 512], start=True, stop=True),
                   r=[wkvb, lat], w=[pb])
                op("dve", lambda e: e.tensor_copy(out=kb.ap[0:64, g * 512:(g + 1) * 512], in_=pb.ap[0:64, :]),
                   r=[pb], w=[kb])
            ch.append(f)
        for g8 in range(0, NB, 8):
            def f(g8=g8):
                pb = PB[pcnt[0] % 2]
                pcnt[0] += 1
                nn = min(8, NB - g8)
                for i in range(nn):
                    n = g8 + i
                    op("pe", lambda e: e.matmul(pb.ap[:, i * 64:(i + 1) * 64], lhsT=lat.ap[:, n * 128:(n + 1) * 128],
                                                rhs=wkvb.ap[:, 0, h * 128 + 64:h * 128 + 128], start=True, stop=True),
                       r=[wkvb, lat], w=[pb], inc=(i == nn - 1))
                op("dve", lambda e: e.tensor_copy(out=vb.ap[:, g8:g8 + nn, 0:64],
                                                  in_=pb.ap[:, 0:nn * 64].rearrange("p (n d) -> p n d", d=64)),
                   r=[pb], w=[vb])
            ch.append(f)
        for G in range(NG):
            def f(G=G):
                tsl = slice(G * GN, (G + 1) * GN)
                qa_ = qa2[G % 2]
                qb_ = qb2[G % 2]
                for (pb, M, wsel) in ((B[0], 64, 0), (B[1], 32, 1), (B[1], 32, 2)):
                    for k in range(2):
                        if wsel == 0:
                            l_ap, rl = wqb.ap[:, k, h * 96:h * 96 + 64], wqb
                        elif wsel == 1:
                            l_ap, rl = wqb.ap[:, k, h * 96 + 64:h * 96 + 96], wqb
                        else:
                            l_ap, rl = wqr.ap[:, k, h, :], wqr
                        p0 = 64 if wsel == 2 else 0
                        op("pe", lambda e: e.matmul(pb.ap[p0:p0 + M, 0:GN], lhsT=l_ap, rhs=cqn.ap[:, k, tsl], start=(k == 0),
                                                    stop=(k == 1)), r=[rl, cqn], w=[pb], inc=(k == 1))
                op("dve", lambda e: e.tensor_copy(out=qt.ap[0:64, tsl], in_=B[0].ap[0:64, 0:GN]), r=[B[0]], w=[qt])
                op("dve", lambda e: e.tensor_tensor(out=qa_.ap[0:32, :], in0=B[1].ap[0:32, 0:GN], in1=cosO.ap[0:32, tsl],
                                                    op=ALU.mult), r=[B[1], cosO], w=[qa_])
                op("dve", lambda e: e.tensor_tensor(out=qb_.ap[0:32, :], in0=B[1].ap[64:96, 0:GN], in1=sinO.ap[0:32, tsl],
                                                    op=ALU.mult), r=[B[1], sinO], w=[qb_])
                op("dve", lambda e: e.tensor_tensor(out=qt.ap[64:96, tsl], in0=qa_.ap[0:32, :], in1=qb_.ap[0:32, :],
                                                    op=ALU.add), r=[qa_, qb_], w=[qt])
            ch.append(f)
        return ch

    for f in prod_chunks(0):
        f()
    og = 0
    units_per_head = sum(4 * (GT * G + GT - 1) + 4 for G in range(NG))
    for h in range(8):
        Kb = Kb2[h % 2]
        Vb = Vb2[h % 2]
        QT = QT2[h % 2]
        pend = prod_chunks(h + 1) if h + 1 < 8 else []
        every = max(1, units_per_head // (len(pend) + 1)) if pend else 0
        ucount = 0
        for G in range(NG):
            nlast = 4 * (GT * G + GT - 1) + 3
            pbO = OB[og % 2]
            rr_ = rrB[og % 2]
            og += 1
            steps = list(range(nlast + 1))

            def amin(n):
                return max(0, -((-(n - 3)) // 4) - GT * G)

            def emit_S(n):
                a0 = amin(n)
                pbS = SB[n % 4]
                c0 = a0 * 128
                masks = []
                for a in range(a0, GT):
                    i = n - 4 * (GT * G + a)
                    if 0 <= i <= 3:
                        masks.append((a, i))
                op("pe", lambda e: e.matmul(pbS.ap[:, c0:GN], lhsT=Kb.ap[0:96, n * 128:(n + 1) * 128],
                                            rhs=QT.ap[0:96, G * GN + c0:(G + 1) * GN], start=True, stop=(not masks)),
                   r=[Kb, QT], w=[pbS], inc=(not masks))
                for j, (a, i) in enumerate(masks):
                    last = (j == len(masks) - 1)
                    op("pe", lambda e, a=a, i=i, last=last: e.matmul(pbS.ap[:, a * 128:(a + 1) * 128], lhsT=ident_b.ap,
                                                                    rhs=mmask.ap[:, i, :], start=False, stop=last),
                       r=[ident_b, mmask], w=[pbS], inc=last)
                pt = PT[n % NPT]
                op("act", lambda e: e.activation(out=pt.ap[:, c0:GN], in_=pbS.ap[:, c0:GN], func=AF.Exp),
                   r=[pbS], w=[pt])

            def emit_PV(n):
                a0 = amin(n)
                c0 = a0 * 128
                pt = PT[n % NPT]
                op("pe", lambda e: e.matmul(pbO.ap[:, c0:GN], lhsT=Vb.ap[:, n, :], rhs=pt.ap[:, c0:GN],
                                            start=(n == 0), stop=(n == nlast)), r=[Vb, pt], w=[pbO], inc=True)

            LAG = 4
            for s_ in range(len(steps) + LAG):
                if s_ < len(steps):
                    emit_S(steps[s_])
                    ucount += 1
                    if pend and ucount % every == 0:
                        pend.pop(0)()
                if s_ >= LAG:
                    emit_PV(steps[s_ - LAG])
            hp = (h % 2) * 64
            op("dve", lambda e: e.reciprocal(out=rr_.ap[64:128, 0:GN], in_=pbO.ap[64:128, 0:GN]), r=[pbO], w=[rr_])
            op("dve", lambda e: e.tensor_tensor(out=outB.ap[hp:hp + 64, h // 2, G * GN:(G + 1) * GN],
                                                in0=pbO.ap[0:64, 0:GN], in1=rr_.ap[64:128, 0:GN], op=ALU.mult),
               r=[pbO, rr_], w=[outB])
        while pend:
            pend.pop(0)()
    K.barrier()
    K.release(MIX_MARK)
    K.top = CONST_MARK + 2 * ((4 * NT * 2 + 63) // 64 * 64)
    E_MARK = K.mark()

    hTa = K.alloc([128, 8, NT], BF16, "hTa")
    mT = K.alloc([128, 8, NT], BF16, "mT")
    stage = [K.alloc([128, 8, 256], F32, "stage") for _ in range(2)]
    sa = [K.alloc([128, 512], F32, "sa") for _ in range(2)]
    sb_ = [K.alloc([128, 512], F32, "sb") for _ in range(2)]
    t1 = [K.alloc([128, 512], F32, "t1") for _ in range(2)]
    t2 = [K.alloc([128, 512], F32, "t2") for _ in range(2)]

    def wset():
        return (K.alloc([128, 8, 512], BF16, "wga"), K.alloc([128, 8, 512], BF16, "wgb"),
                K.alloc([128, 4, 512], BF16, "woa"), K.alloc([128, 4, 512], BF16, "wob"))

    def wset_chunks(ws, half, ctr):
        ch = []
        ch += load_w(ws[0], w_in, 0, 8, half * 512, 512, gain=g1c, chunk=256, stage=stage, defer=True, ctr=ctr)
        ch += load_w(ws[1], w_in, 0, 8, 1024 + half * 512, 512, gain=g1c, chunk=256, stage=stage, defer=True, ctr=ctr)
        ch += load_w(ws[2], w_o_a, 0, 4, half * 512, 512, chunk=256, stage=stage, defer=True, ctr=ctr)
        ch += load_w(ws[3], w_o_b, 0, 4, half * 512, 512, chunk=256, stage=stage, defer=True, ctr=ctr)
        return ch

    sctr = [0]
    ws0 = wset()
    m1 = K.mark()
    xts = [K.alloc([128, D], F32, "xt") for _ in range(4)]
    xss = [K.alloc([128, D], BF16, "xs") for _ in range(4)]
    stat = K.alloc([128, 2 * GT], F32, "stat")
    pend0 = wset_chunks(ws0, 0, sctr)
    for G in range(NG):
        hv = T(hTa.ap[:, :, G * GN:(G + 1) * GN], "hv")
        hv.b = hTa.b
        n_ = nt_front([("dram", xo[(G * GT + a) * 128:(G * GT + a + 1) * 128, :]) for a in range(GT)], xts, xss, stat)
        for _ in range(2):
            if pend0:
                pend0.pop(0)()
        nt_back(n_, hv, xss, [B[0], B[1]])
    while pend0:
        pend0.pop(0)()
    K.barrier()
    K.release(m1)
    ws1 = wset()
    it = 0
    for half in range(2):
        wga, wgb, woa, wob = ws0 if half == 0 else ws1
        pend = wset_chunks(ws1, 1, sctr) if half == 0 else []
        for G in range(NG):
            tsl = slice(G * GN, (G + 1) * GN)
            for il in range(4):
                i = half * 4 + il
                bs = [B[0], B[1], B[2], B[3]] if it % 2 == 0 else [B[4], B[5], B[6], B[7]]
                u = it % 2
                it += 1
                csl = slice(il * 128, (il + 1) * 128)
                for (pb, wt, src, nk) in ((bs[0], wga, hTa, 8), (bs[1], wgb, hTa, 8), (bs[2], woa, outA, 4),
                                          (bs[3], wob, outB, 4)):
                    for k in range(nk):
                        op("pe", lambda e, k=k, pb=pb, wt=wt, src=src, nk=nk, csl=csl, tsl=tsl: e.matmul(
                            pb.ap[:, 0:GN], lhsT=wt.ap[:, k, csl], rhs=src.ap[:, k, tsl], start=(k == 0),
                            stop=(k == nk - 1)), r=[wt, src], w=[pb], inc=(k == nk - 1))
                op("act", lambda e, u=u, bs=bs: e.activation(out=sa[u].ap[:, 0:GN], in_=bs[0].ap[:, 0:GN],
                                                            func=AF.Sigmoid), r=[bs[0]], w=[sa[u]])
                op("act", lambda e, u=u, bs=bs: e.activation(out=sb_[u].ap[:, 0:GN], in_=bs[1].ap[:, 0:GN],
                                                            func=AF.Sigmoid), r=[bs[1]], w=[sb_[u]])
                op("dve", lambda e, u=u, bs=bs: e.tensor_tensor(out=t1[u].ap[:, 0:GN], in0=bs[2].ap[:, 0:GN],
                                                               in1=sa[u].ap[:, 0:GN], op=ALU.mult),
                   r=[bs[2], sa[u]], w=[t1[u]])
                op("dve", lambda e, u=u, bs=bs: e.tensor_tensor(out=t2[u].ap[:, 0:GN], in0=bs[3].ap[:, 0:GN],
                                                               in1=sb_[u].ap[:, 0:GN], op=ALU.mult),
                   r=[bs[3], sb_[u]], w=[t2[u]])
                op("dve", lambda e, u=u, i=i, tsl=tsl: e.tensor_tensor(out=mT.ap[:, i, tsl], in0=t1[u].ap[:, 0:GN],
                                                                       in1=t2[u].ap[:, 0:GN], op=ALU.add),
                   r=[t1[u], t2[u]], w=[mT])
                if pend:
                    pend.pop(0)()
        while pend:
            pend.pop(0)()
    K.barrier()

    K.top = E_MARK + 2 * 8 * NT * 2
    woh = [K.alloc([128, 8, 512], BF16, "wo") for _ in range(2)]
    stage = [K.alloc([128, 8, 256], F32, "stage") for _ in range(2)]
    for hf in range(2):
        load_w(woh[hf], w_out, 0, 8, hf * 512, 512, chunk=256, stage=stage)
    junk = K.alloc([128, 512], BF16, "junk")
    st2 = [K.alloc([128, 4], F32, "st2") for _ in range(4)]
    ytmp = [K.alloc([128, 512], F32, "ytmp") for _ in range(4)]
    xsl = [K.alloc([128, D], F32, "xsl") for _ in range(4)]

    def post_norm_residual(pbs, gb, res, u, junk, st2, ytmp):
        s = st2[u]
        for hf in range(2):
            op("act", lambda e, hf=hf: e.activation(out=junk.ap, in_=pbs[hf].ap, func=AF.Square,
                                                    accum_out=s.ap[:, hf:hf + 1]), r=[pbs[hf]], w=[junk, s])
        op("act", lambda e: e.activation(out=s.ap[:, 2:3], in_=s.ap[:, 0:1], func=AF.Identity, bias=s.ap[:, 1:2]),
           r=[s], w=[s])
        op("act", lambda e: e.activation(out=s.ap[:, 3:4], in_=s.ap[:, 2:3], func=AF.Ln, bias=EPS, scale=1.0 / D),
           r=[s], w=[s])
        op("act", lambda e: e.activation(out=s.ap[:, 3:4], in_=s.ap[:, 3:4], func=AF.Exp, scale=-0.5), r=[s], w=[s])
        for hf in range(2):
            yt = ytmp[(2 * u + hf) % len(ytmp)]
            op("dve", lambda e, hf=hf, yt=yt: e.scalar_tensor_tensor(
                out=yt.ap, in0=pbs[hf].ap, scalar=s.ap[:, 3:4], in1=gb.ap[:, hf * 512:(hf + 1) * 512], op0=ALU.mult,
                op1=ALU.mult), r=[pbs[hf], s, gb], w=[yt])
            op("dve", lambda e, hf=hf, yt=yt: e.tensor_tensor(out=res.ap[:, hf * 512:(hf + 1) * 512], in0=yt.ap,
                                                             in1=res.ap[:, hf * 512:(hf + 1) * 512], op=ALU.add),
               r=[yt, res], w=[res])

    for m in range(NQ):
        pbs = [B[2 * (m % 4)], B[2 * (m % 4) + 1]]
        xs_ = xsl[m % 4]
        K.dma("sp", xs_.ap, xo[m * 128:(m + 1) * 128, :], w=[xs_])
        for hf in range(2):
            for k in range(8):
                op("pe", lambda e, k=k, hf=hf, m=m: e.matmul(pbs[hf].ap, lhsT=mT.ap[:, k, m * 128:(m + 1) * 128],
                                                            rhs=woh[hf].ap[:, k, :], start=(k == 0),
                                                            stop=(k == 7)), r=[mT, woh[hf]], w=[pbs[hf]], inc=(k == 7))
        post_norm_residual(pbs, g2b, xs_, m % 4, junk, st2, ytmp)
        K.dma("act", out[m * 128:(m + 1) * 128, :], xs_.ap, r=[xs_])
    K.barrier()

    K.top = F_BASE
    TG = NT // 2 if NT >= 1024 else NT
    NP = NT // TG
    SN = min(512, TG)
    NS = TG // SN
    NTL = TG // 128
    aT = K.alloc([128, 32, TG], BF16, "aT")
    h2T = K.alloc([128, 8, TG], BF16, "h2T")
    wd = K.alloc([128, 32, D], BF16, "wd")
    WD_END = K.mark()
    K.top = WD_END - 32 * D * 2
    xts = [K.alloc([128, D], F32, "xt") for _ in range(8)]
    xss = [K.alloc([128, D], BF16, "xs") for _ in range(8)]
    statF = [K.alloc([128, 8], F32, "stat") for _ in range(2)]
    assert K.top <= WD_END
    K.top = WD_END
    UP_MARK = K.mark()
    wst = [K.alloc([128, 8, 256], F32, "wst") for _ in range(2)]
    wuc = [K.alloc([128, 8, 256], BF16, "wuc") for _ in range(2)]
    sqf = [K.alloc([128, 512], F32, "sqf") for _ in range(3)]
    dstg = [K.alloc([128, D], F32, "dstg") for _ in range(2)]
    K.top = UP_MARK
    junk = K.alloc([128, 512], BF16, "junk")
    st2 = [K.alloc([128, 4], F32, "st2") for _ in range(2)]
    ytmp = [K.alloc([128, 512], F32, "ytmp") for _ in range(4)]
    xsl = [K.alloc([128, D], F32, "xsl") for _ in range(3)]
    UB = [B[2], B[3], B[4], B[5], B[6], B[7]]
    NCH = 16
    for ps in range(NP):
        tiles = list(range(ps * NTL, (ps + 1) * NTL))
        fr = []
        for gi, c0 in enumerate(range(0, len(tiles), 4)):
            sub = tiles[c0:c0 + 4]
            hv = T(h2T.ap[:, :, c0 * 128:(c0 + len(sub)) * 128], "hv")
            hv.b = h2T.b
            xt_ = xts[4 * (gi % 2):4 * (gi % 2) + 4]
            xs_g = xss[4 * (gi % 2):4 * (gi % 2) + 4]
            n_ = nt_front([("dram", out[m * 128:(m + 1) * 128, :]) for m in sub], xt_, xs_g, statF[gi % 2])
            fr.append((n_, hv, xs_g))
            if len(fr) == 2:
                for (n2, hv2, xs2) in fr:
                    nt_back(n2, hv2, xs2, [B[0], B[1]])
                fr = []
        for (n2, hv2, xs2) in fr:
            nt_back(n2, hv2, xs2, [B[0], B[1]])
        K.barrier()

        def wup_dma(ch):
            st = wst[ch % 2]
            K.dma("sp", st.ap, w_up[:, ch * 256:(ch + 1) * 256].rearrange("(k p) c -> p k c", p=128), w=[st])

        def wup_cast(ch):
            wc = wuc[ch % 2]
            st = wst[ch % 2]
            op("dve", lambda e: e.tensor_tensor(out=wc.ap[:, 0:5, :], in0=st.ap[:, 0:5, :],
                                                in1=g3c.ap[:, 0:5, None].to_broadcast([128, 5, 256]), op=ALU.mult),
               r=[st, g3c], w=[wc])
            for k in range(5, 8):
                op("act", lambda e, k=k: e.activation(out=wc.ap[:, k, :], in_=st.ap[:, k, :], func=AF.Copy,
                                                      scale=g3c.ap[:, k:k + 1]), r=[st, g3c], w=[wc])

        def wd_dma(k):
            st = dstg[k % 2]
            K.dma("act", st.ap, w_down[k * 128:(k + 1) * 128, :], w=[st])

        def wd_cast(k):
            st = dstg[k % 2]
            if k % 2 == 0:
                op("dve", lambda e: e.tensor_copy(out=wd.ap[:, k, :], in_=st.ap), r=[st], w=[wd])
            else:
                op("act", lambda e: e.activation(out=wd.ap[:, k, :], in_=st.ap, func=AF.Copy), r=[st], w=[wd])

        wup_dma(0)
        wup_dma(1)
        wup_cast(0)
        wd_dma(0)
        wd_dma(1)
        ui = 0
        for ch in range(NCH):
            wc = wuc[ch % 2]
            if ch + 1 < NCH:
                wup_cast(ch + 1)
            if ch + 2 < NCH:
                wup_dma(ch + 2)
            for il in range(2):
                i = ch * 2 + il
                wd_cast(i)
                if i + 2 < 32:
                    wd_dma(i + 2)
                for s_ in range(NS):
                    pb = UB[ui % 6]
                    sf = sqf[ui % 3]
                    ui += 1
                    for k in range(8):
                        op("pe", lambda e, k=k, pb=pb, wc=wc, il=il, s_=s_: e.matmul(
                            pb.ap[:, 0:SN], lhsT=wc.ap[:, k, il * 128:(il + 1) * 128],
                            rhs=h2T.ap[:, k, s_ * SN:(s_ + 1) * SN], start=(k == 0), stop=(k == 7)),
                           r=[wc, h2T], w=[pb], inc=(k == 7))
                    op("act", lambda e, pb=pb, sf=sf: e.activation(out=sf.ap[:, 0:SN], in_=pb.ap[:, 0:SN], func=AF.Square),
                       r=[pb], w=[sf])
                    op("dve", lambda e, pb=pb, sf=sf, i=i, s_=s_: e.scalar_tensor_tensor(
                        out=aT.ap[:, i, s_ * SN:(s_ + 1) * SN], in0=pb.ap[:, 0:SN], scalar=0.0, in1=sf.ap[:, 0:SN],
                        op0=ALU.is_gt, op1=ALU.mult), r=[pb, sf], w=[aT])
        K.barrier()
        for ti, m in enumerate(tiles):
            pbs = [B[0], B[1]] if ti % 2 == 0 else [B[2], B[3]]
            xs_ = xsl[ti % 3]
            K.dma("sp", xs_.ap, out[m * 128:(m + 1) * 128, :], w=[xs_])
            for hf in range(2):
                for k in range(32):
                    op("pe", lambda e, k=k, hf=hf, ti=ti: e.matmul(pbs[hf].ap, lhsT=aT.ap[:, k, ti * 128:(ti + 1) * 128],
                                                                  rhs=wd.ap[:, k, hf * 512:(hf + 1) * 512],
                                                                  start=(k == 0), stop=(k == 31)),
                       r=[aT, wd], w=[pbs[hf]], inc=(k == 31))
            post_norm_residual(pbs, g4b, xs_, ti % 2, junk, st2, ytmp)
            K.dma("act", out[m * 128:(m + 1) * 128, :], xs_.ap, r=[xs_])
        K.barrier()
    print("kernel build: sems=%d" % K.nsem)
    K.finish()
    return nc


def host_inputs(S, x, positions, weights):
    NB = S // 128
    NQ = NB // 4
    NT = NQ * 128
    in_maps = []
    freqs = (10000.0 ** (-np.arange(0, 32, 2, dtype=np.float32) / 32)).astype(np.float32)
    freqrow = np.ascontiguousarray(np.broadcast_to(freqs[None, :], (128, 16))).astype(np.float32)
    slopes = (2.0 ** (-8.0 * np.arange(1, 9, dtype=np.float32) / 8)).astype(np.float32)
    nslope = np.ascontiguousarray(np.broadcast_to(-slopes[None, :], (128, 8))).astype(np.float32)
    kk = np.arange(128)[:, None]
    qq = np.arange(128)[None, :]
    tri = np.where(kk <= qq, 0.0, NEG).astype(np.float32)
    allneg = np.full((128, 128), NEG, np.float32)
    zero = np.zeros((128, 128), np.float32)
    sw_prev = np.where(kk > qq, 0.0, BIG).astype(np.float32)
    sw_cur = np.where(kk <= qq, 0.0, BIG).astype(np.float32)
    sw_all = np.full((128, 128), BIG, np.float32)
    for c in range(8):
        b, j = c // 4, c % 4
        own = [4 * m + j for m in range(NQ)]
        prev = [max(g - 1, 0) for g in own]
        xb = np.ascontiguousarray(x[b])
        xbl = xb.reshape(NB, 128, D)
        pos = positions[b].astype(np.int32)
        posl = pos.reshape(NB, 128)
        mm = np.stack([zero if i < j else (tri if i == j else allneg) for i in range(4)], axis=1)
        sm = np.stack([sw_all if j == 0 else sw_prev, sw_prev, sw_cur], axis=1)
        posk = np.empty((128, 2 * NQ), np.int32)
        for m in range(NQ):
            posk[:, 2 * m] = posl[prev[m]]
            posk[:, 2 * m + 1] = posl[own[m]]
        d = {
            "xb": xb,
            "xo": np.ascontiguousarray(xbl[own].reshape(NT, D)),
            "xp": np.ascontiguousarray(xbl[prev].reshape(NT, D)),
            "posbc": np.ascontiguousarray(posl.T),
            "posoc": np.ascontiguousarray(posl[own].T),
            "poso": np.ascontiguousarray(posl[own].reshape(NT)),
            "posk": posk,
            "mlamask": np.ascontiguousarray(mm),
            "swamask": np.ascontiguousarray(sm),
            "freqrow": freqrow,
            "nslope": nslope,
        }
        d.update(weights)
        in_maps.append(d)
    return in_maps


_NC_CACHE = {}


def run(S, x, positions, weights):
    NB = S // 128
    NQ = NB // 4
    if S not in _NC_CACHE:
        _NC_CACHE[S] = build(S)
    nc = _NC_CACHE[S]
    in_maps = host_inputs(S, x, positions, weights)
    res = run_bass_kernel_spmd(nc, in_maps, core_ids=list(range(8)))
    B_ = x.shape[0]
    outp = np.empty((B_, S, D), np.float32)
    ov = outp.reshape(B_, NB, 128, D)
    for c in range(8):
        b, j = c // 4, c % 4
        o = np.asarray(res.results[c]["out"]).reshape(NQ, 128, D)
        for m in range(NQ):
            ov[b, 4 * m + j] = o[m]
    return outp


def kernel(x, positions, pre_norm_mix, w_in, q_a_norm, w_q_b, kv_a_norm, w_kv_b, sinks, w_o_a, w_o_b, w_out,
           post_norm_mix, pre_norm_mlp, w_up, w_down, post_norm_mlp):
    f = lambda a: np.ascontiguousarray(np.asarray(a, dtype=np.float32)[0])
    weights = {
        "w_in": f(w_in), "w_q_b": f(w_q_b), "w_kv_b": f(w_kv_b), "w_o_a": f(w_o_a), "w_o_b": f(w_o_b),
        "w_out": f(w_out), "w_up": f(w_up), "w_down": f(w_down),
        "pre_norm_mix": f(pre_norm_mix), "q_a_norm": f(q_a_norm), "kv_a_norm": f(kv_a_norm), "sinks": f(sinks),
        "post_norm_mix": f(post_norm_mix), "pre_norm_mlp": f(pre_norm_mlp), "post_norm_mlp": f(post_norm_mlp),
    }
    x = np.asarray(x, dtype=np.float32)
    positions = np.asarray(positions)
    return run(x.shape[1], x, positions, weights)
```
